# Optimizing a Trainium2 kernel written in Bass

```python
import math
import jax, jax.numpy as jnp
from jax import lax
import numpy as np


D_MODEL = 1024
BATCH = 8
SEQ = 2048
DEPTH = 2

D_MIX = D_MODEL
D_CONV = D_MIX // 4
D_LRU = D_MIX // 4
D_ATTN = D_MIX // 2
CONV_KERNEL = 31
LRU_HEADS = 4
LRU_HEAD_DIM = D_LRU // LRU_HEADS
LRU_CONV = 4
LRU_C = 8.0
HEAD_DIM = 64
N_Q_HEADS = D_ATTN // HEAD_DIM
N_KV_HEADS = 2
GQA_REP = N_Q_HEADS // N_KV_HEADS
KV_DIM = N_KV_HEADS * HEAD_DIM
WINDOW = 128
BLOCK = 128
REL_BUCKETS = 32
REL_MAX_DIST = 128
D_FF = 2816
ALPHA = (2.0 * DEPTH) ** 0.25
BETA = (8.0 * DEPTH) ** -0.25
LN_EPS = 1e-5
D_IN = 2 * D_CONV + 2 * D_LRU + D_ATTN + 2 * KV_DIM

kernel_name = 'hymba_conv_rglru_swa_macaron_deepnorm'


def _layernorm(x, g, b):
    xf = x.astype(jnp.float32)
    mu = jnp.mean(xf, axis=-1, keepdims=True)
    var = jnp.mean(jnp.square(xf - mu), axis=-1, keepdims=True)
    return ((xf - mu) * lax.rsqrt(var + LN_EPS)).astype(x.dtype) * g + b


def _swiglu(x, wg, wu, wd):
    return (jax.nn.silu(x @ wg) * (x @ wu)) @ wd


def _causal_dw_conv(x, w):
    k = w.shape[0]
    return lax.conv_general_dilated(
        x, w[:, None, :].astype(x.dtype), window_strides=(1,), padding=[(k - 1, 0)],
        dimension_numbers=('NWC', 'WIO', 'NWC'), feature_group_count=x.shape[-1])


def _conv_module(u, dw_w, dw_b, g, b):
    a, gate = jnp.split(u, 2, axis=-1)
    y = a * jax.nn.sigmoid(gate)
    y = _causal_dw_conv(y, dw_w) + dw_b
    y = _layernorm(y, g, b)
    return jax.nn.silu(y)


def _lin_combine(left, right):
    a1, b1 = left
    a2, b2 = right
    return a1 * a2, a2 * b1 + b2


def _recurrent(u, conv_w, conv_b, wa, ba, wx, bx, lam):
    B, S, _ = u.shape
    xb, gb = jnp.split(u, 2, axis=-1)
    xb = _causal_dw_conv(xb, conv_w) + conv_b
    xh = xb.reshape(B, S, LRU_HEADS, LRU_HEAD_DIM)
    r = jax.nn.sigmoid(jnp.einsum('bshi,hij->bshj', xh, wa).reshape(B, S, D_LRU) + ba)
    i = jax.nn.sigmoid(jnp.einsum('bshi,hij->bshj', xh, wx).reshape(B, S, D_LRU) + bx)
    log_a = LRU_C * r.astype(jnp.float32) * jax.nn.log_sigmoid(lam.astype(jnp.float32))
    a = jnp.exp(log_a)
    mult = jnp.sqrt(-jnp.expm1(2.0 * log_a))
    bterm = mult * (i * xb).astype(jnp.float32)
    _, h = lax.associative_scan(_lin_combine, (a, bterm), axis=1)
    return h.astype(u.dtype) * jax.nn.gelu(gb)


def _rel_bucket(dist):
    max_exact = REL_BUCKETS // 2
    is_small = dist < max_exact
    large = max_exact + (jnp.log(jnp.maximum(dist, 1).astype(jnp.float32) / max_exact)
                         / math.log(REL_MAX_DIST / max_exact)
                         * (REL_BUCKETS - max_exact)).astype(jnp.int32)
    large = jnp.minimum(large, REL_BUCKETS - 1)
    return jnp.where(is_small, dist, large)


def _band_bias_and_mask(rel_bias, seq):
    nb = seq // BLOCK
    qi = jnp.arange(BLOCK)[:, None]
    kj = jnp.arange(2 * BLOCK)[None, :]
    dist = qi - kj + BLOCK
    bucket = _rel_bucket(jnp.maximum(dist, 0))
    bias = rel_bias[bucket].astype(jnp.float32)
    bias = jnp.transpose(bias, (2, 0, 1)).reshape(N_KV_HEADS, GQA_REP, BLOCK, 2 * BLOCK)
    blk = jnp.arange(nb)[:, None, None]
    valid = (dist >= 0) & (dist < WINDOW) & (blk * BLOCK + kj - BLOCK >= 0)
    return bias, valid


def _band(t, nb):
    B = t.shape[0]
    tp = jnp.pad(t, ((0, 0), (BLOCK, 0), (0, 0), (0, 0)))
    tp = tp.reshape(B, nb + 1, BLOCK, t.shape[2], t.shape[3])
    return jnp.concatenate([tp[:, :-1], tp[:, 1:]], axis=2)


def _swa(q, k, v, band_bias, valid, sinks):
    B, S, _ = q.shape
    nb = S // BLOCK
    qb = q.reshape(B, nb, BLOCK, N_KV_HEADS, GQA_REP, HEAD_DIM)
    kb = _band(k.reshape(B, S, N_KV_HEADS, HEAD_DIM), nb)
    vb = _band(v.reshape(B, S, N_KV_HEADS, HEAD_DIM), nb)
    s = jnp.einsum('bnqgrd,bnkgd->bngrqk', qb, kb).astype(jnp.float32) * (HEAD_DIM ** -0.5)
    s = s + band_bias[None, None]
    s = jnp.where(valid[None, :, None, None], s, -1e30)
    sink = jnp.broadcast_to(sinks.astype(jnp.float32).reshape(1, 1, N_KV_HEADS, GQA_REP, 1, 1),
                            s.shape[:-1] + (1,))
    p = jax.nn.softmax(jnp.concatenate([s, sink], axis=-1), axis=-1)[..., :-1]
    o = jnp.einsum('bngrqk,bnkgd->bnqgrd', p.astype(vb.dtype), vb)
    return o.reshape(B, S, D_ATTN)


def _mixer(x, w_in, conv_dw_w, conv_dw_b, conv_ln_g, conv_ln_b, lru_conv_w, lru_conv_b,
           lru_wa, lru_ba, lru_wx, lru_bx, lru_lambda, sinks, w_out, band_bias, valid):
    u = x @ w_in
    o1 = 2 * D_CONV
    o2 = o1 + 2 * D_LRU
    o3 = o2 + D_ATTN
    o4 = o3 + KV_DIM
    u_conv, u_lru, q, k, v = jnp.split(u, [o1, o2, o3, o4], axis=-1)
    y_conv = _conv_module(u_conv, conv_dw_w, conv_dw_b, conv_ln_g, conv_ln_b)
    y_lru = _recurrent(u_lru, lru_conv_w, lru_conv_b, lru_wa, lru_ba, lru_wx, lru_bx, lru_lambda)
    y_attn = _swa(q, k, v, band_bias, valid, sinks)
    return jnp.concatenate([y_conv, y_lru, y_attn], axis=-1) @ w_out


def setup_inputs(seed: int = 0) -> dict:
    key = jax.random.key(seed)
    ks = jax.random.split(key, 24)
    f32 = jnp.float32
    nrm = lambda k, shape, scale: jax.random.normal(k, shape, f32) * scale
    x = nrm(ks[0], (BATCH, SEQ, D_MODEL), 1.0)
    rel_bias = nrm(ks[1], (REL_BUCKETS, N_Q_HEADS), 0.5)
    ln_g = 1.0 + nrm(ks[2], (DEPTH, 3, D_MODEL), 0.1)
    ln_b = nrm(ks[3], (DEPTH, 3, D_MODEL), 0.02)
    ffn_w_gate = nrm(ks[4], (DEPTH, 2, D_MODEL, D_FF), D_MODEL ** -0.5)
    ffn_w_up = nrm(ks[5], (DEPTH, 2, D_MODEL, D_FF), D_MODEL ** -0.5)
    ffn_w_down = nrm(ks[6], (DEPTH, 2, D_FF, D_MODEL), BETA * D_FF ** -0.5)
    w_in = nrm(ks[7], (DEPTH, D_MODEL, D_IN), D_MODEL ** -0.5)
    conv_dw_w = nrm(ks[8], (DEPTH, CONV_KERNEL, D_CONV), CONV_KERNEL ** -0.5)
    conv_dw_b = nrm(ks[9], (DEPTH, D_CONV), 0.02)
    conv_ln_g = 1.0 + nrm(ks[10], (DEPTH, D_CONV), 0.1)
    conv_ln_b = nrm(ks[11], (DEPTH, D_CONV), 0.02)
    lru_conv_w = nrm(ks[12], (DEPTH, LRU_CONV, D_LRU), LRU_CONV ** -0.5)
    lru_conv_b = nrm(ks[13], (DEPTH, D_LRU), 0.02)
    lru_wa = nrm(ks[14], (DEPTH, LRU_HEADS, LRU_HEAD_DIM, LRU_HEAD_DIM), LRU_HEAD_DIM ** -0.5)
    lru_ba = nrm(ks[15], (DEPTH, D_LRU), 0.02)
    lru_wx = nrm(ks[16], (DEPTH, LRU_HEADS, LRU_HEAD_DIM, LRU_HEAD_DIM), LRU_HEAD_DIM ** -0.5)
    lru_bx = nrm(ks[17], (DEPTH, D_LRU), 0.02)
    a_init = jax.random.uniform(ks[18], (DEPTH, D_LRU), f32, 0.9, 0.999)
    sig = a_init ** (1.0 / LRU_C)
    lru_lambda = jnp.log(sig) - jnp.log1p(-sig)
    attn_sinks = nrm(ks[19], (DEPTH, N_Q_HEADS), 0.5)
    w_out = nrm(ks[20], (DEPTH, D_MIX, D_MODEL), BETA * D_MIX ** -0.5)
    return {'x': x, 'rel_bias': rel_bias, 'ln_g': ln_g, 'ln_b': ln_b,
            'ffn_w_gate': ffn_w_gate, 'ffn_w_up': ffn_w_up, 'ffn_w_down': ffn_w_down,
            'w_in': w_in, 'conv_dw_w': conv_dw_w, 'conv_dw_b': conv_dw_b,
            'conv_ln_g': conv_ln_g, 'conv_ln_b': conv_ln_b,
            'lru_conv_w': lru_conv_w, 'lru_conv_b': lru_conv_b,
            'lru_wa': lru_wa, 'lru_ba': lru_ba, 'lru_wx': lru_wx, 'lru_bx': lru_bx,
            'lru_lambda': lru_lambda, 'attn_sinks': attn_sinks, 'w_out': w_out}


def reference(x, rel_bias, ln_g, ln_b, ffn_w_gate, ffn_w_up, ffn_w_down, w_in,
              conv_dw_w, conv_dw_b, conv_ln_g, conv_ln_b, lru_conv_w, lru_conv_b,
              lru_wa, lru_ba, lru_wx, lru_bx, lru_lambda, attn_sinks, w_out):
    band_bias, valid = _band_bias_and_mask(rel_bias, x.shape[1])
    for l in range(DEPTH):
        h = 0.5 * _swiglu(x, ffn_w_gate[l, 0], ffn_w_up[l, 0], ffn_w_down[l, 0])
        x = _layernorm(ALPHA * x + h, ln_g[l, 0], ln_b[l, 0])
        h = _mixer(x, w_in[l], conv_dw_w[l], conv_dw_b[l], conv_ln_g[l], conv_ln_b[l],
                   lru_conv_w[l], lru_conv_b[l], lru_wa[l], lru_ba[l], lru_wx[l], lru_bx[l],
                   lru_lambda[l], attn_sinks[l], w_out[l], band_bias, valid)
        x = _layernorm(ALPHA * x + h, ln_g[l, 1], ln_b[l, 1])
        h = 0.5 * _swiglu(x, ffn_w_gate[l, 1], ffn_w_up[l, 1], ffn_w_down[l, 1])
        x = _layernorm(ALPHA * x + h, ln_g[l, 2], ln_b[l, 2])
    return x
```

```python
import contextlib
import numpy as np
import concourse.bass as bass
import concourse.mybir as mybir
from concourse.bass_utils import run_bass_kernel_spmd

F32 = mybir.dt.float32
BF16 = mybir.dt.bfloat16
F32R = mybir.dt.float32r
AF = mybir.ActivationFunctionType
ALU = mybir.AluOpType

D = 1024
S = 2048
DEPTH = 2
DFF = 2816
NCH = DFF // 128
KC = D // 128
TB = 512
NB = S // TB
ALPHA = (2.0 * DEPTH) ** 0.25
EPS = 1e-5
GROUPS = [[0, 1, 2, 3], [4, 5, 6, 7], [8, 9, 10], [11, 12, 13], [14, 15, 16, 17], [18, 19, 20, 21]]
SLOT_E = 3072
NSLOT = 8
UNITS_PER_LAYER = 2 * NCH + 5 + 3 + 3
DIAG_SLOTS = [0, 1, 2]
WIN_STREAM = [3, 7]
WOUT_SLOTS = [4, 5, 6]
NPL = 140
P_LNG, P_LNB, P_CW, P_CB, P_CG, P_CBT, P_LW, P_LB, P_BA, P_BX, P_LAM, P_SINK = (
    0, 24, 48, 110, 112, 114, 116, 124, 126, 128, 130, 132)
GELU_C = 0.7978845608028654 * 2.0
SCHED_VERBOSE = False
SIM_ONLY = False
SIM_NF32 = 0
SIM_NBF = 0
SIM_NBANK = 8
SIM_YMIX2 = 0


class Res:
    __slots__ = ("name", "w", "r", "const")

    def __init__(self, name):
        self.name = name
        self.w = None
        self.r = []
        self.const = False


class Op:
    __slots__ = ("i", "eng", "emit", "deps", "cost", "dsem", "pos", "fin", "crit", "start", "tab")

    def __init__(self, i, eng, emit, deps, cost, dsem):
        self.i, self.eng, self.emit, self.deps, self.cost, self.dsem = i, eng, emit, deps, cost, dsem
        self.pos = None
        self.fin = None


ENGS = ("pe", "act", "dve", "pool", "sp")
ACT_TABS = {"Silu": "silu", "Sigmoid": "sig", "Exp": "exp", "Sqrt": "sqrt", "Ln": "exp"}
TAB_COST = 1.3
FIFO_ENGS = ("sp",)
LN_POOL = 1
SPLIT_STREAM = 1
POOL_TT_COST = 1.45
POOL_DMA_ISSUE = 1.2
SCHED_WINDOW = 20
CP_PRIO = 1
SCHED_LAT = 0.25
STRICT_SAME_ENGINE = 1
DMA_RATE = 170e3
DMA_LAT = 3.0


class Sched:
    def __init__(self, nc, stack):
        self.nc = nc
        self.ops = []
        self.sem = {e: stack.enter_context(nc.semaphore("s_" + e)) for e in ENGS}
        self.dsem = {}
        self.stack = stack
        self.marks = []

    def mark(self, name):
        self.marks.append((name, len(self.ops)))

    def new_dsem(self, name):
        self.dsem[name] = self.stack.enter_context(self.nc.semaphore("d_" + name))

    def _record(self, eng, emit, reads, writes, cost, dsem):
        i = len(self.ops)
        deps = {}
        for r in reads:
            if r.w is not None:
                deps[r.w] = True
        for w in writes:
            if w.w is not None:
                deps.setdefault(w.w, False)
            for t in w.r:
                deps.setdefault(t, False)
        deps.pop(i, None)
        o_ = Op(i, eng, emit, deps, cost, dsem)
        o_.tab = None
        if eng == "act":
            for nm in emit.__code__.co_names:
                if nm in ACT_TABS:
                    o_.tab = ACT_TABS[nm]
                    break
        self.ops.append(o_)
        for r in reads:
            if not r.const:
                r.r.append(i)
        for w in writes:
            w.w = i
            w.r = []
        return i

    def op(self, eng, emit, reads=(), writes=(), cost=None, n=512):
        if cost is None:
            if eng == "dve":
                cost = 0.12 + n / 960.0
            elif eng == "act":
                cost = 0.22 + n / 1200.0
            else:
                cost = 0.5
        return self._record(eng, emit, reads, writes, cost, None)

    def dma(self, qeng, emit, dname, reads=(), writes=(), nbytes=262144):
        return self._record(qeng, emit, reads, writes, nbytes / DMA_RATE, dname)

    def schedule(self):
        ops = self.ops
        n = len(ops)
        pend = {e: [] for e in ENGS}
        for o in ops:
            pend[o.eng].append(o.i)
        head = {e: 0 for e in ENGS}
        done = [False] * n
        tfree = {e: 0.0 for e in ENGS}
        dma_free = 0.0
        order = {e: [] for e in ENGS}
        remaining = n
        cur_tab = None
        tail = [0.0] * n
        if CP_PRIO:
            dependents = [[] for _ in range(n)]
            for o in ops:
                for d in o.deps:
                    dependents[d].append(o.i)
            for i in range(n - 1, -1, -1):
                m = 0.0
                for j in dependents[i]:
                    if tail[j] > m:
                        m = tail[j]
                tail[i] = ops[i].cost + (DMA_LAT if ops[i].dsem is not None else 0.0) + m
        while remaining:
            best = None
            for e in ENGS:
                lst = pend[e]
                h = head[e]
                while h < len(lst) and done[lst[h]]:
                    h += 1
                head[e] = h
                if h >= len(lst):
                    continue
                lim = 1 if e in FIFO_ENGS else SCHED_WINDOW
                cnt = 0
                j = h
                te = tfree[e]
                while j < len(lst) and cnt < lim:
                    idx = lst[j]
                    j += 1
                    if done[idx]:
                        continue
                    cnt += 1
                    o = ops[idx]
                    ready = te
                    ok = True
                    crit = -1
                    for d, raw in o.deps.items():
                        od = ops[d]
                        f = od.fin
                        if f is None:
                            ok = False
                            break
                        if od.eng != e or od.dsem is not None or ((raw or STRICT_SAME_ENGINE) and e != "pe"):
                            f += SCHED_LAT
                        if f > ready:
                            ready = f
                            crit = d
                    if not ok:
                        continue
                    if o.tab is not None and o.tab != cur_tab:
                        ready += TAB_COST
                    o.crit = crit
                    if CP_PRIO:
                        key = (max(ready, te), -tail[idx], idx)
                        if best is None or key < best[2]:
                            best = ((ready, idx), e, key)
                    else:
                        if best is None or (ready, idx) < best[0]:
                            best = ((ready, idx), e, None)
                        if ready <= te:
                            break
            assert best is not None, "scheduler stuck"
            (ready, idx), e = best[0], best[1]
            o = ops[idx]
            if o.dsem is not None:
                tfree[e] = ready + (POOL_DMA_ISSUE if e == "pool" else 0.06)
                st_ = max(ready, dma_free)
                o.fin = st_ + o.cost + DMA_LAT
                dma_free = st_ + o.cost
            else:
                if o.tab is not None:
                    cur_tab = o.tab
                o.fin = ready + o.cost
                tfree[e] = o.fin
            o.start = ready
            if o.crit == -1 and order[e]:
                o.crit = -2 - order[e][-1]
            done[idx] = True
            order[e].append(idx)
            remaining -= 1
        self.order = order
        self.makespan = max(o.fin for o in ops)

    def _assign(self):
        ops = self.ops
        pos = [0] * len(ops)
        for e in ENGS:
            for p, idx in enumerate(self.order[e]):
                pos[idx] = p + 1
        self.waits = {}
        needed = set()
        for e in ENGS:
            waited = {}
            for idx in self.order[e]:
                o = ops[idx]
                strict = o.dsem is not None
                wl = []
                for d, raw in o.deps.items():
                    od = ops[d]
                    if od.dsem is None and od.eng == e and not strict:
                        if e == "pe" or (not raw and not STRICT_SAME_ENGINE):
                            continue
                    key = ("d_" + od.dsem) if od.dsem is not None else od.eng
                    if waited.get(key, 0) >= pos[d]:
                        continue
                    waited[key] = pos[d]
                    wl.append(d)
                    needed.add(d)
                self.waits[idx] = wl
        cnt = {e: 0 for e in ENGS}
        dcnt = {k: 0 for k in self.dsem}
        tok = [None] * len(ops)
        for e in ENGS:
            for idx in self.order[e]:
                o = ops[idx]
                if o.dsem is not None:
                    dcnt[o.dsem] += 16
                    tok[idx] = (self.dsem[o.dsem], dcnt[o.dsem])
                elif idx in needed:
                    cnt[e] += 1
                    tok[idx] = (self.sem[e], cnt[e])
        self.tok = tok
        self.dcnt_final = dcnt
        self.ninc = len(needed)

    def emit_all_one(self, e, eng):
        if not hasattr(self, "tok"):
            self._assign()
        ops = self.ops
        tok = self.tok
        for idx in self.order[e]:
            o = ops[idx]
            for d in self.waits[idx]:
                sem, val = tok[d]
                eng.wait_ge(sem, val)
            ins = o.emit(eng)
            if tok[idx] is not None:
                ins.then_inc(tok[idx][0], 16 if o.dsem is not None else 1)
        if e == "sp":
            for k, v in self.dcnt_final.items():
                if k == "out" and v > 0:
                    eng.wait_ge(self.dsem[k], v)


class Pool:
    def __init__(self, nc, stack, name, n, shape, dt):
        self.tiles = [stack.enter_context(nc.sbuf_tensor(f"{name}{i}", shape, dt)) for i in range(n)]
        self.res = [Res(f"{name}{i}") for i in range(n)]
        self.gen = [0] * n
        self.i = 0
        self.n = n

    def get(self):
        i = self.i
        self.i = (self.i + 1) % self.n
        self.gen[i] += 1
        return TRef(self, i, self.gen[i])


class TRef:
    __slots__ = ("pool", "i", "g")

    def __init__(self, pool, i, g):
        self.pool, self.i, self.g = pool, i, g

    @property
    def t(self):
        return self.pool.tiles[self.i]

    @property
    def r(self):
        assert self.pool.gen[self.i] == self.g, "stale scratch tile"
        return self.pool.res[self.i]


def build_program(n_layers, stop_after=None, skip=()):
    nc = bass.Bass("TRN2", target_bir_lowering=False)
    NU = n_layers * UNITS_PER_LAYER
    xin = nc.dram_tensor("xin", [D, S], F32, kind="ExternalInput").ap()
    wst = nc.dram_tensor("wst", [NU, 128, SLOT_E], F32, kind="ExternalInput").ap()
    prm = nc.dram_tensor("prm", [128, n_layers * NPL], F32, kind="ExternalInput").ap()
    bdd = nc.dram_tensor("bdd", [128, n_layers * 512], F32, kind="ExternalInput").ap()
    bias_d = nc.dram_tensor("biasT", [128, 2048], F32, kind="ExternalInput").ap()
    mask_d = nc.dram_tensor("mask", [128, 256], F32, kind="ExternalInput").ap()
    ident_d = nc.dram_tensor("ident", [128, 128], F32, kind="ExternalInput").ap()
    yout = nc.dram_tensor("yout", [D, S], F32, kind="ExternalOutput").ap()
    ydbg = nc.dram_tensor("ydbg", [D, S], F32, kind="ExternalOutput").ap() if stop_after == "mixer" else None

    with contextlib.ExitStack() as st:
        sc = Sched(nc, st)
        for i in range(NSLOT):
            sc.new_dsem(f"slot{i}")
        sc.new_dsem("const")
        for sl_ in WIN_STREAM:
            for j_ in range(3):
                sc.new_dsem(f"w{sl_}_{j_}")
        for k in range(KC):
            sc.new_dsem(f"xin{k}")
        sc.new_dsem("out")
        for l in range(n_layers):
            sc.new_dsem(f"bd{l}")

        def sb(name, shape, dt):
            return st.enter_context(nc.sbuf_tensor(name, shape, dt))

        xr = sb("xr", [128, KC, S if not SIM_NF32 else S // 4], F32)
        xb = sb("xb", [128, KC, S], BF16)
        ring = sb("ring", [128, NSLOT, SLOT_E], BF16)
        hy = sb("hy", [128, 8, TB], BF16)
        params = sb("params", [128, n_layers * NPL], F32)
        dparams = sb("dparams", [128, n_layers * 64], F32)
        bdb = sb("bdb", [128, n_layers * 512], BF16)
        EB = sb("EB", [128, 2048], BF16)
        maskt = sb("maskt", [128, 256], F32)
        ident = sb("ident_s", [128, 128], F32)
        ones = sb("ones_s", [128, 128], F32R)
        ones_f = sb("ones_f", [128, 128], F32)
        identb = sb("identb", [128, 128], BF16)
        qsb = sb("qsb", [128, 4, TB], BF16)
        ksb = sb("ksb", [128, S], BF16)
        vsb = sb("vsb", [128, 16, 2, 65], BF16)
        ybuf = sb("ybuf", [128, 2, 30 + TB], BF16)
        xbuf = sb("xbuf", [128, 2, 3 + TB], BF16)
        hlast = sb("hlast", [128, 2], F32)
        rden = sb("rden", [128, 8], F32)
        banks = [st.enter_context(nc.psum_tensor(f"bank{i}", [128, 512], F32)) for i in range(8)]
        if SIM_NBANK > 8:
            banks = [banks[i % 8] for i in range(SIM_NBANK)]

        nbf = SIM_NBF or 4
        pb16 = Pool(nc, st, "tb", nbf, [128, TB], BF16)
        rdp = Pool(nc, st, "rd", 4, [128, 8], F32)
        pfr = Pool(nc, st, "tr", 3, [128, TB], F32R)
        nfree = nc.sbuf_bytes_remaining
        nf32 = SIM_NF32 or min(14, (nfree - 1024) // 2048)
        assert nf32 >= 7, f"not enough SBUF for scratch tiles: {nf32}"
        pf32 = Pool(nc, st, "tf", nf32, [128, TB], F32)

        R_xr = [[Res(f"xr{k}_{s}") for s in range(NB)] for k in range(KC)]
        R_xb = [[Res(f"xb{k}_{s}") for s in range(NB)] for k in range(KC)]
        R_slot = [Res(f"slot{i}") for i in range(NSLOT)]
        R_sub = {sl: [Res(f"sub{sl}_{j}") for j in range(3)] for sl in WIN_STREAM}

        def SR(slot):
            return R_sub[slot] if (SPLIT_STREAM and slot in R_sub) else [R_slot[slot]]
        R_hy = [Res(f"hy{i}") for i in range(8)]
        R_bank = [Res(f"bank{i}") for i in range(SIM_NBANK)]
        R_ym2 = [[Res(f"ym{p}_{i}") for i in range(8)] for p in range(2)]

        def RY(s, c):
            return R_ym2[s % 2][c] if SIM_YMIX2 else R_hy[c]
        R_const = Res("const")
        R_dpar = Res("dpar")
        R_qsb = Res("qsb")
        R_ksb = [Res(f"ksb{s}") for s in range(NB)]
        R_vsb = [Res(f"vsb{s}") for s in range(NB)]
        R_ybuf = [Res("ybuf0"), Res("ybuf1")]
        R_xbuf = [Res("xbuf0"), Res("xbuf1")]
        R_hlast = [Res("hlast0"), Res("hlast1")]
        R_rden = Res("rden")
        R_out = Res("out")
        bank_i = [0]

        def nbank():
            i = bank_i[0]
            bank_i[0] = (i + 1) % SIM_NBANK
            return i

        def blk(s):
            return slice(s * TB, (s + 1) * TB)

        xin_v = xin.rearrange("(k p) t -> p k t", p=128)
        for s in range(NB):
            sc.dma("sp", lambda e, s=s: e.dma_start(out=xr[:, :, blk(s)], in_=xin_v[:, :, blk(s)]),
                   f"xin{s}", writes=[R_xr[k][s] for k in range(KC)], nbytes=128 * KC * TB * 4)

        stg0 = pf32.get()
        stg1 = pf32.get()
        sc.dma("sp", lambda e: e.dma_start(out=params[:], in_=prm[:, :]), "const", writes=[R_const])
        ebst = [pf32.get() for _ in range(4)]
        for qi_ in range(4):
            sc.new_dsem(f"eb{qi_}")
            sc.dma("sp", lambda e, qi_=qi_: e.dma_start(out=ebst[qi_].t[:], in_=bias_d[:, qi_ * 512:(qi_ + 1) * 512]),
                   f"eb{qi_}", writes=[ebst[qi_].r])
        sc.dma("sp", lambda e: e.dma_start(out=maskt[:], in_=mask_d[:, :]), "const", writes=[R_const])
        for l in range(n_layers):
            tl = stg0 if l == 0 else stg1
            sc.dma("sp", lambda e, tl=tl, l=l: e.dma_start(out=tl.t[:], in_=bdd[:, l * 512:(l + 1) * 512]),
                   f"bd{l}", writes=[tl.r])
        def load_unit(unit, slot):
            def emit(e, unit=unit, slot=slot):
                return e.dma_start(
                    out=ring[:, slot, :].rearrange("p (a b) -> p a b", a=2),
                    in_=wst[unit].rearrange("p (a b) -> p a b", a=2))
            sc.dma("pool", emit, f"slot{slot}", writes=SR(slot), nbytes=128 * SLOT_E * 4)

        def load_ffn_group(l, f, g):
            for i, j in enumerate(GROUPS[g]):
                load_unit(l * UNITS_PER_LAYER + f * NCH + j, (g % 2) * 4 + i)

        win_loaded = {}
        win_n = [0]

        def ensure_win(l, s, p):
            if s >= NB or p >= 5:
                return
            if (l, s, p) in win_loaded:
                return
            slot = WIN_STREAM[win_n[0] % 2]
            win_n[0] += 1
            win_loaded[(l, s, p)] = slot
            unit = l * UNITS_PER_LAYER + 2 * NCH + p
            if not SPLIT_STREAM:
                load_unit(unit, slot)
                return
            for loc in range(3 if p < 4 else 2):
                def emit(e, unit=unit, slot=slot, loc=loc):
                    return e.dma_start(out=ring[:, slot, loc * 1024:(loc + 1) * 1024],
                                       in_=wst[unit][:, loc * 1024:(loc + 1) * 1024])
                sc.dma("pool", emit, f"w{slot}_{loc}", writes=[R_sub[slot][loc]], nbytes=128 * 1024 * 4)

        def load_diag(l):
            for p in range(3):
                load_unit(l * UNITS_PER_LAYER + 2 * NCH + 8 + p, DIAG_SLOTS[p])

        def load_wout(l):
            for p in range(3):
                load_unit(l * UNITS_PER_LAYER + 2 * NCH + 5 + p, WOUT_SLOTS[p])

        load_ffn_group(0, 0, 0)
        load_ffn_group(0, 0, 1)

        sc.op("dve", lambda e: e.memset(ones_f[:], 1.0), writes=[R_const])
        sc.op("dve", lambda e: e.tensor_copy(ones[:], ones_f[:]), reads=[R_const], writes=[R_const])
        sc.new_dsem("ident")
        sc.dma("sp", lambda e: e.dma_start(out=ident[:], in_=ident_d[:, :]), "ident", reads=[R_const], writes=[R_const])
        sc.op("act", lambda e: e.activation(identb[:], ident[:], AF.Identity), reads=[R_const], writes=[R_const])
        sc.op("dve", lambda e: e.memset(vsb[:, :, :, 64:65], 1.0), writes=R_vsb)
        for qi_ in range(4):
            sc.op("act", lambda e, qi_=qi_: e.activation(ebst[qi_].t[:], ebst[qi_].t[:], AF.Exp),
                  reads=[ebst[qi_].r], writes=[ebst[qi_].r])
            for h2 in range(2):
                sc.op("dve", lambda e, qi_=qi_, h2=h2: e.tensor_tensor(
                    EB[:, qi_ * 512 + h2 * 256:qi_ * 512 + (h2 + 1) * 256],
                    ebst[qi_].t[:, h2 * 256:(h2 + 1) * 256], maskt[:], ALU.mult),
                    reads=[ebst[qi_].r, R_const], writes=[R_const])
        for l in range(n_layers):
            tl = stg0 if l == 0 else stg1
            sc.op("act", lambda e, tl=tl, l=l: e.activation(bdb[:, l * 512:(l + 1) * 512], tl.t[:], AF.Identity),
                  reads=[tl.r], writes=[R_const])
        for l in range(n_layers):
            po = l * NPL
            do = l * 64
            sc.op("dve", lambda e, po=po, do=do: e.tensor_scalar(
                dparams[:, do:do + 48], params[:, po:po + 48], float(ALPHA), None, ALU.mult),
                reads=[R_const], writes=[R_dpar])
            sc.op("act", lambda e, po=po, do=do: e.activation(
                dparams[:, do + 48:do + 50], params[:, po + P_LAM:po + P_LAM + 2], AF.Exp, scale=-1.0),
                reads=[R_const], writes=[R_dpar])
            sc.op("act", lambda e, do=do: e.activation(
                dparams[:, do + 48:do + 50], dparams[:, do + 48:do + 50], AF.Ln, bias=1.0),
                reads=[R_dpar], writes=[R_dpar])
            sc.op("dve", lambda e, do=do: e.tensor_scalar(
                dparams[:, do + 50:do + 52], dparams[:, do + 48:do + 50], -16.0, None, ALU.mult),
                reads=[R_dpar], writes=[R_dpar])
            sc.op("dve", lambda e, do=do: e.tensor_scalar(
                dparams[:, do + 48:do + 50], dparams[:, do + 48:do + 50], -8.0, None, ALU.mult),
                reads=[R_dpar], writes=[R_dpar])
            sc.op("act", lambda e, po=po, do=do: e.activation(
                dparams[:, do + 52:do + 60], params[:, po + P_SINK:po + P_SINK + 8], AF.Exp),
                reads=[R_const], writes=[R_dpar])
        for k in range(KC):
            for s in range(NB):
                sc.op("act", lambda e, k=k, s=s: e.activation(xb[:, k, blk(s)], xr[:, k, blk(s)], AF.Identity),
                      reads=[R_xr[k][s]], writes=[R_xb[k][s]])
                sc.op("dve", lambda e, k=k, s=s: e.tensor_scalar(
                    xr[:, k, blk(s)], xr[:, k, blk(s)], float(ALPHA), None, ALU.mult),
                    reads=[R_xr[k][s]], writes=[R_xr[k][s]])

        R_const.const = True
        R_dpar.const = True

        def ln_block(l, i, s, final):
            po = l * NPL
            do = l * 64
            b1 = nbank()
            b2 = nbank()
            for k in range(KC):
                xc = pfr.get()
                sc.op("act", lambda e, xc=xc, k=k: e.activation(xc.t[:], xr[:, k, blk(s)], AF.Identity),
                      reads=[R_xr[k][s]], writes=[xc.r])
                sc.op("pe", lambda e, xc=xc, k=k: e.matmul(banks[b1][:], ones[:], xc.t[:],
                                                            start=(k == 0), stop=(k == KC - 1)),
                      reads=[xc.r, R_const], writes=[R_bank[b1]], cost=0.23)
                sq = pfr.get()
                sc.op("act", lambda e, sq=sq, k=k: e.activation(sq.t[:], xr[:, k, blk(s)], AF.Square),
                      reads=[R_xr[k][s]], writes=[sq.r])
                sc.op("pe", lambda e, sq=sq, k=k: e.matmul(banks[b2][:], ones[:], sq.t[:],
                                                            start=(k == 0), stop=(k == KC - 1)),
                      reads=[sq.r, R_const], writes=[R_bank[b2]], cost=0.23)
            mean = pf32.get()
            sc.op("dve", lambda e: e.tensor_scalar(mean.t[:], banks[b1][:], 1.0 / D, None, ALU.mult),
                  reads=[R_bank[b1]], writes=[mean.r])
            msq = pf32.get()
            sc.op("dve", lambda e: e.tensor_tensor(msq.t[:], mean.t[:], mean.t[:], ALU.mult),
                  reads=[mean.r], writes=[msq.r])
            sc.op("dve", lambda e: e.scalar_tensor_tensor(msq.t[:], banks[b2][:], 1.0 / D, msq.t[:],
                                                          ALU.mult, ALU.subtract),
                  reads=[R_bank[b2], msq.r], writes=[msq.r])
            sc.op("act", lambda e: e.activation(msq.t[:], msq.t[:], AF.Ln, bias=EPS),
                  reads=[msq.r], writes=[msq.r])
            sc.op("act", lambda e: e.activation(msq.t[:], msq.t[:], AF.Exp, scale=-0.5),
                  reads=[msq.r], writes=[msq.r])
            rstd = msq
            sc.op("dve", lambda e: e.tensor_tensor(mean.t[:], mean.t[:], rstd.t[:], ALU.mult),
                  reads=[mean.r, rstd.r], writes=[mean.r])
            mrs = mean
            for k in range(KC):
                if LN_POOL:
                    sc.op("pool", lambda e, k=k: e.tensor_tensor(xr[:, k, blk(s)], xr[:, k, blk(s)], rstd.t[:], ALU.mult),
                          reads=[R_xr[k][s], rstd.r], writes=[R_xr[k][s]], cost=POOL_TT_COST)
                else:
                    sc.op("dve", lambda e, k=k: e.tensor_tensor(xr[:, k, blk(s)], xr[:, k, blk(s)], rstd.t[:], ALU.mult),
                          reads=[R_xr[k][s], rstd.r], writes=[R_xr[k][s]])
                sc.op("dve", lambda e, k=k: e.tensor_tensor(xr[:, k, blk(s)], xr[:, k, blk(s)], mrs.t[:], ALU.subtract),
                      reads=[R_xr[k][s], mrs.r], writes=[R_xr[k][s]])
                gc = po + P_LNG + i * 8 + k
                bc = po + P_LNB + i * 8 + k
                if final:
                    sc.op("act", lambda e, k=k, gc=gc, bc=bc: e.activation(
                        xr[:, k, blk(s)], xr[:, k, blk(s)], AF.Identity, bias=params[:, bc:bc + 1],
                        scale=params[:, gc:gc + 1]),
                        reads=[R_xr[k][s], R_const], writes=[R_xr[k][s]])
                    sc.dma("sp", lambda e, k=k: e.dma_start(out=yout[k * 128:(k + 1) * 128, blk(s)],
                                                            in_=xr[:, k, blk(s)]),
                           "out", reads=[R_xr[k][s]], writes=[R_out])
                else:
                    ag = do + i * 8 + k
                    ab = do + 24 + i * 8 + k
                    sc.op("act", lambda e, k=k, gc=gc, bc=bc: e.activation(
                        xb[:, k, blk(s)], xr[:, k, blk(s)], AF.Identity, bias=params[:, bc:bc + 1],
                        scale=params[:, gc:gc + 1]),
                        reads=[R_xr[k][s], R_const], writes=[R_xb[k][s]])
                    sc.op("dve", lambda e, k=k, ag=ag, ab=ab: e.tensor_scalar(
                        xr[:, k, blk(s)], xr[:, k, blk(s)], dparams[:, ag:ag + 1], dparams[:, ab:ab + 1],
                        ALU.mult, ALU.add),
                        reads=[R_xr[k][s], R_dpar], writes=[R_xr[k][s]])

        def _mm_group(e, lst):
            ins = None
            n = len(lst)
            for i, (o, a, b) in enumerate(lst):
                ins = e.matmul(o, a, b, start=(i == 0), stop=(i == n - 1))
            return ins

        def ffn_up(g, s, hb):
            for jj, j in enumerate(GROUPS[g]):
                slot = (g % 2) * 4 + jj
                bg = nbank()
                sc.op("pe", lambda e, slot=slot, bg=bg: _mm_group(e, [
                    (banks[bg][:], ring[:, slot, k * 128:(k + 1) * 128], xb[:, k, blk(s)]) for k in range(KC)]),
                    reads=SR(slot) + [R_xb[k][s] for k in range(KC)], writes=[R_bank[bg]], cost=1.78)
                bu = nbank()
                sc.op("pe", lambda e, slot=slot, bu=bu: _mm_group(e, [
                    (banks[bu][:], ring[:, slot, 1024 + k * 128:1024 + (k + 1) * 128], xb[:, k, blk(s)])
                    for k in range(KC)]),
                    reads=SR(slot) + [R_xb[k][s] for k in range(KC)], writes=[R_bank[bu]], cost=1.78)
                sg = pb16.get()
                sc.op("act", lambda e, sg=sg, bg=bg: e.activation(sg.t[:], banks[bg][:], AF.Silu),
                      reads=[R_bank[bg]], writes=[sg.r])
                sc.op("dve", lambda e, sg=sg, bu=bu, jj=jj: e.scalar_tensor_tensor(
                    hy[:, hb * 4 + jj, :], sg.t[:], 0.5, banks[bu][:], ALU.mult, ALU.mult),
                    reads=[sg.r, R_bank[bu]], writes=[R_hy[hb * 4 + jj]])

        def ffn_down(g, s, hb):
            ng = len(GROUPS[g])
            for m in range(KC):
                by = nbank()
                sc.op("pe", lambda e, m=m, by=by: _mm_group(e, [
                    (banks[by][:], ring[:, (g % 2) * 4 + jj, 2048 + m * 128:2048 + (m + 1) * 128],
                     hy[:, hb * 4 + jj, :]) for jj in range(ng)]),
                    reads=sum((SR((g % 2) * 4 + jj) for jj in range(ng)), []) + [R_hy[hb * 4 + jj] for jj in range(ng)],
                    writes=[R_bank[by]], cost=0.2225 * ng)
                sc.op("dve", lambda e, m=m, by=by: e.tensor_tensor(
                    xr[:, m, blk(s)], banks[by][:], xr[:, m, blk(s)], ALU.add),
                    reads=[R_bank[by], R_xr[m][s]], writes=[R_xr[m][s]])

        hcount = [0]

        cur_l = [0]

        def u_chunk_ap(cc, k, s):
            p, loc = divmod(cc, 3)
            return win_loaded[(cur_l[0], s, p)], loc * 1024 + k * 128

        def u_prefetch(cc, s):
            p = cc // 3
            ensure_win(cur_l[0], s, p)
            if p + 1 < 5:
                ensure_win(cur_l[0], s, p + 1)
            else:
                ensure_win(cur_l[0], s + 1, 0)

        def u_matmul(cc, s):
            u_prefetch(cc, s)
            b = nbank()
            slot0, _ = u_chunk_ap(cc, 0, s)

            def emit(e, cc=cc, b=b):
                lst = []
                for k in range(KC):
                    slot, off = u_chunk_ap(cc, k, s)
                    lst.append((banks[b][:], ring[:, slot, off:off + 128], xb[:, k, blk(s)]))
                return _mm_group(e, lst)
            sc.op("pe", emit, reads=([R_sub[slot0][cc % 3]] if SPLIT_STREAM else [R_slot[slot0]]) + [R_xb[k][s] for k in range(KC)], writes=[R_bank[b]],
                  cost=1.8)
            return b

        def mixer_block(l, s):
            cur_l[0] = l
            po = l * NPL
            do = l * 64
            bo = l * 512
            if 'conv' not in skip:
              _conv_branch(l, s, po, do, bo)
            if 'lru' not in skip:
              _lru_branch(l, s, po, do, bo)
            if 'attn' not in skip:
              _attn_branch(l, s, po, do, bo)

        def _conv_branch(l, s, po, do, bo):
            ba = [u_matmul(0, s), u_matmul(1, s)]
            bg = [u_matmul(2, s), u_matmul(3, s)]
            accs = []
            for c in range(2):
                sig = pf32.get()
                sc.op("act", lambda e, sig=sig, c=c: e.activation(sig.t[:], banks[bg[c]][:], AF.Sigmoid),
                      reads=[R_bank[bg[c]]], writes=[sig.r])
                sc.op("dve", lambda e, sig=sig, c=c: e.tensor_tensor(
                    ybuf[:, c, 30:30 + TB], banks[ba[c]][:], sig.t[:], ALU.mult),
                    reads=[R_bank[ba[c]], sig.r], writes=[R_ybuf[c]])
                acc = pf32.get()
                cb = po + P_CB + c
                bc = nbank()

                def emit_conv(e, c=c, bc=bc):
                    ins = None
                    for j in range(31):
                        i = c * 31 + j
                        ins = e.matmul(banks[bc][:], ring[:, DIAG_SLOTS[i // 24], (i % 24) * 128:(i % 24 + 1) * 128],
                                       ybuf[:, c, j:j + TB], start=(j == 0), stop=(j == 30))
                    return ins
                sc.op("pe", emit_conv, reads=[R_slot[x] for x in DIAG_SLOTS] + [R_ybuf[c]], writes=[R_bank[bc]],
                      cost=31 * 0.222)
                sc.op("act", lambda e, acc=acc, bc=bc, cb=cb: e.activation(
                    acc.t[:], banks[bc][:], AF.Identity, bias=params[:, cb:cb + 1]),
                    reads=[R_bank[bc], R_const], writes=[acc.r])
                sc.op("dve", lambda e, c=c: e.tensor_copy(ybuf[:, c, 0:30], ybuf[:, c, TB:TB + 30]),
                      reads=[R_ybuf[c]], writes=[R_ybuf[c]], n=30)
                accs.append(acc)
            b1 = nbank()
            b2 = nbank()
            for c in range(2):
                ar = pfr.get()
                sc.op("act", lambda e, ar=ar, c=c: e.activation(ar.t[:], accs[c].t[:], AF.Identity),
                      reads=[accs[c].r], writes=[ar.r])
                sc.op("pe", lambda e, ar=ar, c=c: e.matmul(banks[b1][:], ones[:], ar.t[:], start=(c == 0), stop=(c == 1)),
                      reads=[ar.r, R_const], writes=[R_bank[b1]], cost=0.23)
                sq = pfr.get()
                sc.op("act", lambda e, sq=sq, c=c: e.activation(sq.t[:], accs[c].t[:], AF.Square),
                      reads=[accs[c].r], writes=[sq.r])
                sc.op("pe", lambda e, sq=sq, c=c: e.matmul(banks[b2][:], ones[:], sq.t[:], start=(c == 0), stop=(c == 1)),
                      reads=[sq.r, R_const], writes=[R_bank[b2]], cost=0.23)
            mean2 = pf32.get()
            sc.op("dve", lambda e: e.tensor_scalar(mean2.t[:], banks[b1][:], 1.0 / 256, None, ALU.mult),
                  reads=[R_bank[b1]], writes=[mean2.r])
            var = pf32.get()
            sc.op("dve", lambda e: e.tensor_tensor(var.t[:], mean2.t[:], mean2.t[:], ALU.mult),
                  reads=[mean2.r], writes=[var.r])
            sc.op("dve", lambda e: e.scalar_tensor_tensor(var.t[:], banks[b2][:], 1.0 / 256, var.t[:],
                                                          ALU.mult, ALU.subtract),
                  reads=[R_bank[b2], var.r], writes=[var.r])
            sc.op("act", lambda e: e.activation(var.t[:], var.t[:], AF.Ln, bias=EPS),
                  reads=[var.r], writes=[var.r])
            sc.op("act", lambda e: e.activation(var.t[:], var.t[:], AF.Exp, scale=-0.5),
                  reads=[var.r], writes=[var.r])
            sc.op("dve", lambda e: e.tensor_tensor(mean2.t[:], mean2.t[:], var.t[:], ALU.mult),
                  reads=[mean2.r, var.r], writes=[mean2.r])
            for c in range(2):
                acc = accs[c]
                sc.op("dve", lambda e, acc=acc: e.tensor_tensor(acc.t[:], acc.t[:], var.t[:], ALU.mult),
                      reads=[acc.r, var.r], writes=[acc.r])
                sc.op("dve", lambda e, acc=acc: e.tensor_tensor(acc.t[:], acc.t[:], mean2.t[:], ALU.subtract),
                      reads=[acc.r, mean2.r], writes=[acc.r])
                cg = po + P_CG + c
                cbt = po + P_CBT + c
                sc.op("act", lambda e, acc=acc, c=c, cg=cg, cbt=cbt: e.activation(
                    hy[:, c, :], acc.t[:], AF.Silu, bias=params[:, cbt:cbt + 1], scale=params[:, cg:cg + 1]),
                    reads=[acc.r, R_const], writes=[RY(s, c)])


        def _lru_branch(l, s, po, do, bo):
            bx = [u_matmul(4, s), u_matmul(5, s)]
            bgb = [u_matmul(6, s), u_matmul(7, s)]
            for c in range(2):
                sc.op("act", lambda e, c=c: e.activation(xbuf[:, c, 3:3 + TB], banks[bx[c]][:], AF.Identity),
                      reads=[R_bank[bx[c]]], writes=[R_xbuf[c]])
                xc = pf32.get()
                lb = po + P_LB + c
                b4 = nbank()

                def emit_c4(e, c=c, b4=b4):
                    ins = None
                    for j in range(4):
                        i = 62 + c * 4 + j
                        ins = e.matmul(banks[b4][:], ring[:, DIAG_SLOTS[i // 24], (i % 24) * 128:(i % 24 + 1) * 128],
                                       xbuf[:, c, j:j + TB], start=(j == 0), stop=(j == 3))
                    return ins
                sc.op("pe", emit_c4, reads=[R_slot[x] for x in DIAG_SLOTS] + [R_xbuf[c]], writes=[R_bank[b4]],
                      cost=4 * 0.222)
                sc.op("act", lambda e, xc=xc, b4=b4, lb=lb: e.activation(
                    xc.t[:], banks[b4][:], AF.Identity, bias=params[:, lb:lb + 1]),
                    reads=[R_bank[b4], R_const], writes=[xc.r])
                sc.op("dve", lambda e, c=c: e.tensor_copy(xbuf[:, c, 0:3], xbuf[:, c, TB:TB + 3]),
                      reads=[R_xbuf[c]], writes=[R_xbuf[c]], n=3)
                xcb = pb16.get()
                sc.op("act", lambda e, xc=xc, xcb=xcb: e.activation(xcb.t[:], xc.t[:], AF.Identity),
                      reads=[xc.r], writes=[xcb.r])
                bra = nbank()
                sc.op("pe", lambda e, xcb=xcb, bra=bra, c=c: e.matmul(
                    banks[bra][:], bdb[:, bo + c * 128:bo + (c + 1) * 128], xcb.t[:], start=True, stop=True),
                    reads=[xcb.r, R_const], writes=[R_bank[bra]], cost=0.25)
                brx = nbank()
                sc.op("pe", lambda e, xcb=xcb, brx=brx, c=c: e.matmul(
                    banks[brx][:], bdb[:, bo + 256 + c * 128:bo + 256 + (c + 1) * 128], xcb.t[:],
                    start=True, stop=True),
                    reads=[xcb.r, R_const], writes=[R_bank[brx]], cost=0.25)
                rr = pf32.get()
                pba = po + P_BA + c
                pbx = po + P_BX + c
                sc.op("act", lambda e, rr=rr, bra=bra, pba=pba: e.activation(
                    rr.t[:], banks[bra][:], AF.Sigmoid, bias=params[:, pba:pba + 1]),
                    reads=[R_bank[bra], R_const], writes=[rr.r])
                ii = pf32.get()
                sc.op("act", lambda e, ii=ii, brx=brx, pbx=pbx: e.activation(
                    ii.t[:], banks[brx][:], AF.Sigmoid, bias=params[:, pbx:pbx + 1]),
                    reads=[R_bank[brx], R_const], writes=[ii.r])
                aa = pf32.get()
                cl = do + 48 + c
                cl2 = do + 50 + c
                sc.op("act", lambda e, aa=aa, rr=rr, cl=cl: e.activation(
                    aa.t[:], rr.t[:], AF.Exp, scale=dparams[:, cl:cl + 1]),
                    reads=[rr.r, R_dpar], writes=[aa.r])
                sc.op("act", lambda e, rr=rr, cl2=cl2: e.activation(
                    rr.t[:], rr.t[:], AF.Exp, scale=dparams[:, cl2:cl2 + 1]),
                    reads=[rr.r, R_dpar], writes=[rr.r])
                sc.op("dve", lambda e, rr=rr: e.tensor_scalar(rr.t[:], rr.t[:], -1.0, 1.0, ALU.mult, ALU.add),
                      reads=[rr.r], writes=[rr.r])
                sc.op("act", lambda e, rr=rr: e.activation(rr.t[:], rr.t[:], AF.Sqrt),
                      reads=[rr.r], writes=[rr.r])
                sc.op("dve", lambda e, ii=ii, xc=xc: e.tensor_tensor(ii.t[:], ii.t[:], xc.t[:], ALU.mult),
                      reads=[ii.r, xc.r], writes=[ii.r])
                sc.op("dve", lambda e, ii=ii, rr=rr: e.tensor_tensor(ii.t[:], ii.t[:], rr.t[:], ALU.mult),
                      reads=[ii.r, rr.r], writes=[ii.r])
                hh = pf32.get()
                if s == 0:
                    sc.op("dve", lambda e, hh=hh, aa=aa, ii=ii: e.tensor_tensor_scan(
                        hh.t[:], aa.t[:], ii.t[:], 0.0, ALU.mult, ALU.add),
                        reads=[aa.r, ii.r], writes=[hh.r])
                else:
                    sc.op("dve", lambda e, hh=hh, aa=aa, ii=ii, c=c: e.tensor_tensor_scan(
                        hh.t[:], aa.t[:], ii.t[:], hlast[:, c:c + 1], ALU.mult, ALU.add),
                        reads=[aa.r, ii.r, R_hlast[c]], writes=[hh.r])
                sc.op("dve", lambda e, hh=hh, c=c: e.tensor_copy(hlast[:, c:c + 1], hh.t[:, TB - 1:TB]),
                      reads=[hh.r], writes=[R_hlast[c]], n=1)
                gsb = aa
                gs = pf32.get()
                sc.op("act", lambda e, gs=gs, c=c: e.activation(gs.t[:], banks[bgb[c]][:], AF.Identity),
                      reads=[R_bank[bgb[c]]], writes=[gs.r])
                t2 = pf32.get()
                sc.op("dve", lambda e, t2=t2, gs=gs: e.tensor_tensor(t2.t[:], gs.t[:], gs.t[:], ALU.mult),
                      reads=[gs.r], writes=[t2.r])
                sc.op("dve", lambda e, t2=t2: e.tensor_scalar(t2.t[:], t2.t[:], 0.044715, 1.0, ALU.mult, ALU.add),
                      reads=[t2.r], writes=[t2.r])
                sc.op("dve", lambda e, t2=t2, gs=gs: e.tensor_tensor(t2.t[:], t2.t[:], gs.t[:], ALU.mult),
                      reads=[t2.r, gs.r], writes=[t2.r])
                sc.op("act", lambda e, t2=t2: e.activation(t2.t[:], t2.t[:], AF.Sigmoid, scale=GELU_C),
                      reads=[t2.r], writes=[t2.r])
                sc.op("dve", lambda e, t2=t2, gs=gs: e.tensor_tensor(t2.t[:], t2.t[:], gs.t[:], ALU.mult),
                      reads=[t2.r, gs.r], writes=[t2.r])
                sc.op("dve", lambda e, t2=t2, hh=hh, c=c: e.tensor_tensor(hy[:, 2 + c, :], t2.t[:], hh.t[:], ALU.mult),
                      reads=[t2.r, hh.r], writes=[RY(s, 2 + c)])


        def _attn_branch(l, s, po, do, bo):
            for c in range(4):
                b = u_matmul(8 + c, s)
                sc.op("act", lambda e, b=b, c=c: e.activation(qsb[:, c, :], banks[b][:], AF.Identity, scale=0.125),
                      reads=[R_bank[b]], writes=[R_qsb])
            b = u_matmul(12, s)
            sc.op("act", lambda e, b=b: e.activation(ksb[:, blk(s)], banks[b][:], AF.Identity),
                  reads=[R_bank[b]], writes=[R_ksb[s]])
            bv = nbank()
            u_prefetch(13, s)
            vslot, voff = u_chunk_ap(13, 0, s)

            def emit_v(e, bv=bv):
                ins = None
                for tt in range(4):
                    T = s * 4 + tt
                    for k in range(KC):
                        slot, off = u_chunk_ap(13, k, s)
                        ins = e.matmul(banks[bv][:, tt * 128:(tt + 1) * 128], xb[:, k, T * 128:(T + 1) * 128],
                                       ring[:, slot, off:off + 128], start=(k == 0), stop=(k == KC - 1))
                return ins
            sc.op("pe", emit_v, reads=([R_sub[vslot][1]] if SPLIT_STREAM else [R_slot[vslot]]) + [R_xb[k][s] for k in range(KC)], writes=[R_bank[bv]], cost=2.2)
            sc.op("dve", lambda e, bv=bv: e.tensor_copy(
                vsb[:, s * 4:(s + 1) * 4, :, 0:64],
                banks[bv][:].rearrange("p (t g d) -> p t g d", t=4, g=2)),
                reads=[R_bank[bv]], writes=[R_vsb[s]])

            for tt in range(4):
                T = s * 4 + tt
                first = (T == 0)
                ob = [nbank(), nbank()]
                kprev_res = R_ksb[(T - 1) // 4] if not first else None
                vprev_res = R_vsb[(T - 1) // 4] if not first else None
                for c0 in (0, 2):
                    sbk = [nbank(), nbank()]

                    def emit_s(e, sbk=sbk, c0=c0, T=T, first=first, tt=tt):
                        ins = None
                        for ci in range(2):
                            for hh_ in range(2):
                                pr = slice(hh_ * 64, (hh_ + 1) * 64)
                                if not first:
                                    ins = e.matmul(banks[sbk[hh_]][:, ci * 256:ci * 256 + 128],
                                                   ksb[pr, (T - 1) * 128:T * 128],
                                                   qsb[pr, c0 + ci, tt * 128:(tt + 1) * 128], start=True, stop=True)
                                ins = e.matmul(banks[sbk[hh_]][:, ci * 256 + 128:ci * 256 + 256],
                                               ksb[pr, T * 128:(T + 1) * 128],
                                               qsb[pr, c0 + ci, tt * 128:(tt + 1) * 128], start=True, stop=True)
                        return ins
                    rds = [R_qsb, R_ksb[s]] + ([kprev_res] if kprev_res is not None else [])
                    sc.op("pe", emit_s, reads=rds, writes=[R_bank[sbk[0]], R_bank[sbk[1]]], cost=0.6)
                    pts = []
                    for hh_ in range(2):
                        ex = pf32.get()
                        pt = pb16.get()
                        eo = hh_ * 1024 + c0 * 256
                        if first:
                            sc.op("act", lambda e, ex=ex, hh_=hh_, sbk=sbk: e.activation(
                                ex.t[:].rearrange("p (h a q) -> p h a q", h=2, a=2)[:, :, 1, :],
                                banks[sbk[hh_]][:].rearrange("p (h a q) -> p h a q", h=2, a=2)[:, :, 1, :], AF.Exp),
                                reads=[R_bank[sbk[hh_]]], writes=[ex.r])
                            sc.op("dve", lambda e, ex=ex, pt=pt, eo=eo: e.tensor_tensor(
                                pt.t[:].rearrange("p (h a q) -> p h a q", h=2, a=2)[:, :, 1, :],
                                ex.t[:].rearrange("p (h a q) -> p h a q", h=2, a=2)[:, :, 1, :],
                                EB[:, eo:eo + 512].rearrange("p (h a q) -> p h a q", h=2, a=2)[:, :, 1, :],
                                ALU.mult),
                                reads=[ex.r, R_const], writes=[pt.r])
                        else:
                            sc.op("act", lambda e, ex=ex, hh_=hh_, sbk=sbk: e.activation(
                                ex.t[:], banks[sbk[hh_]][:], AF.Exp),
                                reads=[R_bank[sbk[hh_]]], writes=[ex.r])
                            sc.op("dve", lambda e, ex=ex, pt=pt, eo=eo: e.tensor_tensor(
                                pt.t[:], ex.t[:], EB[:, eo:eo + 512], ALU.mult),
                                reads=[ex.r, R_const], writes=[pt.r])
                        pts.append(pt)

                    def emit_o(e, pts=pts, c0=c0, T=T, first=first, ob=ob):
                        ins = None
                        for ci in range(2):
                            for hh_ in range(2):
                                o_ap = banks[ob[hh_]][:, (c0 + ci) * 65:(c0 + ci + 1) * 65]
                                pt = pts[hh_]
                                if not first:
                                    e.matmul(o_ap, pt.t[:, ci * 256:ci * 256 + 128], vsb[:, T - 1, hh_, :],
                                             start=True, stop=False)
                                ins = e.matmul(o_ap, pt.t[:, ci * 256 + 128:ci * 256 + 256], vsb[:, T, hh_, :],
                                               start=first, stop=True)
                        return ins
                    rds = [pts[0].r, pts[1].r, R_vsb[s]] + ([vprev_res] if vprev_res is not None else [])
                    sc.op("pe", emit_o, reads=rds, writes=[R_bank[ob[0]], R_bank[ob[1]]], cost=0.45)
                otok = pb16.get()
                rd = rdp.get()
                ovs = []
                for hh_ in range(2):
                    ov = banks[ob[hh_]][:, 0:260].rearrange("p (c d) -> p c d", c=4)
                    ovs.append(ov)
                    sk = do + 52 + hh_ * 4
                    sc.op("dve", lambda e, ov=ov, hh_=hh_, sk=sk, rd=rd: e.tensor_tensor(
                        rd.t[:, hh_ * 4:(hh_ + 1) * 4].rearrange("p (c o) -> p c o", o=1), ov[:, :, 64:65],
                        dparams[:, sk:sk + 4].rearrange("p (c o) -> p c o", o=1), ALU.add),
                        reads=[R_bank[ob[hh_]], R_dpar], writes=[rd.r], n=8)
                sc.op("dve", lambda e, rd=rd: e.reciprocal(rd.t[:], rd.t[:]), reads=[rd.r], writes=[rd.r], n=8)
                for hh_ in range(2):
                    sc.op("dve", lambda e, ov=ovs[hh_], hh_=hh_, otok=otok, rd=rd: e.tensor_tensor(
                        otok.t[:, hh_ * 256:(hh_ + 1) * 256].rearrange("p (c d) -> p c d", c=4), ov[:, :, 0:64],
                        rd.t[:, hh_ * 4:(hh_ + 1) * 4].rearrange("p (c o) -> p c o", o=1).to_broadcast([128, 4, 64]),
                        ALU.mult),
                        reads=[R_bank[ob[hh_]], rd.r], writes=[otok.r], n=256)
                tb = nbank()

                def emit_t(e, otok=otok, tb=tb):
                    ins = None
                    tv = banks[tb][:].bitcast(BF16)
                    for i in range(4):
                        ins = e.transpose(tv[:, i * 128:(i + 1) * 128], otok.t[:, i * 128:(i + 1) * 128],
                                          identb[:])
                    return ins
                sc.op("pe", emit_t, reads=[otok.r, R_const], writes=[R_bank[tb]], cost=0.35)
                sc.op("act", lambda e, tb=tb, tt=tt: e.activation(
                    hy[:, 4:8, tt * 128:(tt + 1) * 128],
                    banks[tb][:].bitcast(BF16)[:, 0:512].rearrange("p (i q) -> p i q", i=4), AF.Identity),
                    reads=[R_bank[tb]], writes=[RY(s, 4), RY(s, 5), RY(s, 6), RY(s, 7)])

        def wout_block(l, s):
            for m in range(KC):
                b = nbank()

                def emit(e, m=m, b=b):
                    lst = []
                    for kc in range(8):
                        p, loc = divmod(kc, 3)
                        lst.append((banks[b][:], ring[:, WOUT_SLOTS[p], loc * 1024 + m * 128:loc * 1024 + (m + 1) * 128],
                                    hy[:, kc, :]))
                    return _mm_group(e, lst)
                sc.op("pe", emit, reads=[R_slot[x] for x in WOUT_SLOTS] + [RY(s, c_) for c_ in range(8)],
                      writes=[R_bank[b]], cost=1.78)
                sc.op("dve", lambda e, m=m, b=b: e.tensor_tensor(
                    xr[:, m, blk(s)], banks[b][:], xr[:, m, blk(s)], ALU.add),
                    reads=[R_bank[b], R_xr[m][s]], writes=[R_xr[m][s]])

        def dump_xr():
            for k in range(KC):
                for s in range(NB):
                    sc.dma("sp", lambda e, k=k, s=s: e.dma_start(out=yout[k * 128:(k + 1) * 128, blk(s)],
                                                                 in_=xr[:, k, blk(s)]),
                           "out", reads=[R_xr[k][s]], writes=[R_out])

        for l in range(n_layers):
            last_layer = (l == n_layers - 1)

            def run_ffn(l, f, ln_i, final, next_loader):
                NG = len(GROUPS)
                pend = None
                for g in range(NG):
                    for s in range(NB):
                        hb = hcount[0] % 2
                        hcount[0] += 1
                        ffn_up(g, s, hb)
                        if pend is not None:
                            ffn_down(*pend)
                            if pend[0] == NG - 1:
                                ln_block(l, ln_i, pend[1], final)
                            if pend[0] == g - 1:
                                next_loader(g - 1)
                        pend = (g, s, hb)
                ffn_down(*pend)
                ln_block(l, ln_i, pend[1], final)
                next_loader(NG - 1)

            def loader_f1(g, l=l):
                if g + 2 < len(GROUPS):
                    load_ffn_group(l, 0, g + 2)
                elif g == len(GROUPS) - 2:
                    ensure_win(l, 0, 0)
                    load_diag(l)
                    if len(GROUPS[-1]) < 4:
                        ensure_win(l, 0, 1)
                else:
                    load_wout(l)

            sc.mark(f"L{l}.ffn1")
            run_ffn(l, 0, 0, False, loader_f1)
            sc.mark(f"L{l}.mixer")
            if stop_after == "ffn1":
                dump_xr()
                break
            sc.op("dve", lambda e: e.memset(ybuf[:, :, 0:30], 0.0), writes=R_ybuf)
            sc.op("dve", lambda e: e.memset(xbuf[:, :, 0:3], 0.0), writes=R_xbuf)
            for s in range(NB):
                mixer_block(l, s)
                if s == NB - 1:
                    load_ffn_group(l, 1, 0)
                wout_block(l, s)
                if ydbg is not None:
                    for c in range(8):
                        sc.dma("pool", lambda e, c=c, s=s: e.dma_start(out=ydbg[c * 128:(c + 1) * 128, blk(s)],
                                                                       in_=hy[:, c, :]),
                               "out", reads=[R_hy[c]], writes=[R_out])
                ln_block(l, 1, s, False)
            if stop_after == "mixer":
                dump_xr()
                break
            load_ffn_group(l, 1, 1)

            def loader_f2(g, l=l):
                if g + 2 < len(GROUPS):
                    load_ffn_group(l, 1, g + 2)
                elif not (l == n_layers - 1):
                    load_ffn_group(l + 1, 0, g + 2 - len(GROUPS))

            sc.mark(f"L{l}.ffn2")
            run_ffn(l, 1, 2, last_layer, loader_f2)

        import time as _time
        _t = _time.time()
        sc.schedule()
        print("sched stats:", {e: len(sc.order[e]) for e in ENGS}, "nf32", nf32,
              "makespan_us %.0f" % sc.makespan, "sched_s %.1f" % (_time.time() - _t))
        if SCHED_VERBOSE:
            mk = sc.marks + [("end", len(sc.ops))]
            for (nm, a), (_, b) in zip(mk[:-1], mk[1:]):
                seg = sc.ops[a:b]
                if not seg:
                    continue
                busy = {e: sum(o.cost for o in seg if o.eng == e and o.dsem is None) for e in ("pe", "act", "dve")}
                print("  phase %-10s start %7.0f end %7.0f  busy" % (nm, min(o.fin - o.cost for o in seg), max(o.fin for o in seg)),
                      {e: round(v) for e, v in busy.items()})
        if SIM_ONLY:
            return nc
        engs = {}
        with nc.Block() as block:
            @block.tensor
            def _(e):
                engs["pe"] = e
                sc.emit_all_one("pe", e)

            @block.vector
            def _(e):
                sc.emit_all_one("dve", e)

            @block.scalar
            def _(e):
                sc.emit_all_one("act", e)

            @block.gpsimd
            def _(e):
                sc.emit_all_one("pool", e)

            @block.sync
            def _(e):
                sc.emit_all_one("sp", e)
    return nc


def _rel_bucket_np(dist):
    max_exact = 16
    d = np.maximum(dist, 1).astype(np.float32)
    large = max_exact + (np.log(d / np.float32(max_exact)) / np.float32(np.log(128 / max_exact))
                         * np.float32(32 - max_exact)).astype(np.int32)
    large = np.minimum(large, 31)
    return np.where(dist < max_exact, dist, large)


def _prep_shared(inp):
    f32 = np.float32
    wst = np.zeros((DEPTH * UNITS_PER_LAYER, 128, SLOT_E), f32)
    qperm = []
    for c in range(4):
        qperm += list(range(1024 + c * 64, 1024 + (c + 1) * 64))
        qperm += list(range(1024 + (4 + c) * 64, 1024 + (5 + c) * 64))
    cols = list(range(1024)) + qperm + list(range(1536, 1792))
    for l in range(DEPTH):
        base = l * UNITS_PER_LAYER
        for f in range(2):
            wg = np.asarray(inp["ffn_w_gate"][l, f], f32).reshape(KC, 128, NCH, 128)
            wu = np.asarray(inp["ffn_w_up"][l, f], f32).reshape(KC, 128, NCH, 128)
            wd = np.asarray(inp["ffn_w_down"][l, f], f32).reshape(NCH, 128, D)
            u = wst[base + f * NCH: base + (f + 1) * NCH]
            u[:, :, 0:1024] = wg.transpose(2, 1, 0, 3).reshape(NCH, 128, 1024)
            u[:, :, 1024:2048] = wu.transpose(2, 1, 0, 3).reshape(NCH, 128, 1024)
            u[:, :, 2048:3072] = wd
        win = np.asarray(inp["w_in"][l], f32)[:, cols].reshape(KC, 128, 14, 128)
        winr = win.transpose(2, 1, 0, 3).reshape(14, 128, 1024)
        for cc in range(14):
            p, loc = divmod(cc, 3)
            wst[base + 2 * NCH + p, :, loc * 1024:(loc + 1) * 1024] = winr[cc]
        wo = np.asarray(inp["w_out"][l], f32).reshape(KC, 128, D)
        for kc in range(KC):
            p, loc = divmod(kc, 3)
            wst[base + 2 * NCH + 5 + p, :, loc * 1024:(loc + 1) * 1024] = wo[kc]
    for l in range(DEPTH):
        base = l * UNITS_PER_LAYER + 2 * NCH + 8
        cw = np.asarray(inp["conv_dw_w"][l], f32)
        ar = np.arange(128)
        for c in range(2):
            for j in range(31):
                i = c * 31 + j
                wst[base + i // 24, ar, (i % 24) * 128 + ar] = cw[j, c * 128:(c + 1) * 128]
        lw = np.asarray(inp["lru_conv_w"][l], f32)
        for c in range(2):
            for j in range(4):
                i = 62 + c * 4 + j
                wst[base + i // 24, ar, (i % 24) * 128 + ar] = lw[j, c * 128:(c + 1) * 128]
    P = np.zeros((128, DEPTH * NPL), f32)
    BD = np.zeros((128, DEPTH * 512), f32)
    for l in range(DEPTH):
        o = l * NPL
        P[:, o + P_LNG:o + P_LNG + 24] = np.asarray(inp["ln_g"][l], f32).reshape(3, 8, 128).transpose(2, 0, 1).reshape(128, 24)
        P[:, o + P_LNB:o + P_LNB + 24] = np.asarray(inp["ln_b"][l], f32).reshape(3, 8, 128).transpose(2, 0, 1).reshape(128, 24)
        P[:, o + P_CW:o + P_CW + 62] = np.asarray(inp["conv_dw_w"][l], f32).reshape(31, 2, 128).transpose(2, 1, 0).reshape(128, 62)
        P[:, o + P_CB:o + P_CB + 2] = np.asarray(inp["conv_dw_b"][l], f32).reshape(2, 128).T
        P[:, o + P_CG:o + P_CG + 2] = np.asarray(inp["conv_ln_g"][l], f32).reshape(2, 128).T
        P[:, o + P_CBT:o + P_CBT + 2] = np.asarray(inp["conv_ln_b"][l], f32).reshape(2, 128).T
        P[:, o + P_LW:o + P_LW + 8] = np.asarray(inp["lru_conv_w"][l], f32).reshape(4, 2, 128).transpose(2, 1, 0).reshape(128, 8)
        P[:, o + P_LB:o + P_LB + 2] = np.asarray(inp["lru_conv_b"][l], f32).reshape(2, 128).T
        P[:, o + P_BA:o + P_BA + 2] = np.asarray(inp["lru_ba"][l], f32).reshape(2, 128).T
        P[:, o + P_BX:o + P_BX + 2] = np.asarray(inp["lru_bx"][l], f32).reshape(2, 128).T
        P[:, o + P_LAM:o + P_LAM + 2] = np.asarray(inp["lru_lambda"][l], f32).reshape(2, 128).T
        P[:, o + P_SINK:o + P_SINK + 8] = np.broadcast_to(np.asarray(inp["attn_sinks"][l], f32)[None, :], (128, 8))
        for ax, nm in enumerate(("lru_wa", "lru_wx")):
            w = np.asarray(inp[nm][l], f32)
            for c in range(2):
                for hh in range(2):
                    BD[hh * 64:(hh + 1) * 64, l * 512 + ax * 256 + c * 128 + hh * 64: l * 512 + ax * 256 + c * 128 + (hh + 1) * 64] = w[2 * c + hh]
    rb = np.asarray(inp["rel_bias"], f32)
    kj = np.arange(128)[:, None]
    qi = np.arange(128)[None, :]
    biasT = np.zeros((128, 2, 4, 2, 128), f32)
    mask = np.zeros((128, 2, 128), f32)
    for part in range(2):
        dist = qi - kj + (128 if part == 0 else 0)
        valid = (dist >= 0) & (dist < 128)
        bucket = _rel_bucket_np(np.maximum(dist, 0))
        mask[:, part, :] = valid.astype(f32)
        for c in range(4):
            for hh in range(2):
                h = c + 4 * hh
                biasT[:, hh, c, part, :] = rb[bucket, h]
    return {
        "wst": wst, "prm": P, "bdd": BD,
        "biasT": np.ascontiguousarray(biasT.reshape(128, 2048)),
        "mask": np.ascontiguousarray(mask.reshape(128, 256)),
        "ident": np.eye(128, dtype=f32),
    }


_NC_CACHE = {}


def _get_nc(n_layers):
    if n_layers not in _NC_CACHE:
        _NC_CACHE[n_layers] = build_program(n_layers)
    return _NC_CACHE[n_layers]


FUSED = True


def kernel(**inputs):
    sh = _prep_shared(inputs)
    x = np.asarray(inputs["x"], np.float32)
    xT = [np.ascontiguousarray(x[b].T) for b in range(8)]
    if FUSED:
        nc = _get_nc(DEPTH)
        in_maps = [dict(sh, xin=xT[b]) for b in range(8)]
        res = run_bass_kernel_spmd(nc, in_maps, core_ids=list(range(8)))
        outs = [res.results[b]["yout"] for b in range(8)]
    else:
        nc = _get_nc(1)
        cur = xT
        for l in range(DEPTH):
            shl = dict(sh)
            shl["wst"] = np.ascontiguousarray(sh["wst"][l * UNITS_PER_LAYER:(l + 1) * UNITS_PER_LAYER])
            shl["prm"] = np.ascontiguousarray(sh["prm"][:, l * NPL:(l + 1) * NPL])
            shl["bdd"] = np.ascontiguousarray(sh["bdd"][:, l * 512:(l + 1) * 512])
            in_maps = [dict(shl, xin=cur[b]) for b in range(8)]
            res = run_bass_kernel_spmd(nc, in_maps, core_ids=list(range(8)))
            cur = [np.ascontiguousarray(res.results[b]["yout"]) for b in range(8)]
        outs = cur
    return np.stack([np.ascontiguousarray(o.T) for o in outs], axis=0).astype(np.float32)
```

```python
import contextlib
import numpy as np
import concourse.bass as bass
import concourse.mybir as mybir
from concourse.bass_utils import run_bass_kernel_spmd

F32 = mybir.dt.float32
BF16 = mybir.dt.bfloat16
F32R = mybir.dt.float32r
AF = mybir.ActivationFunctionType
ALU = mybir.AluOpType

D = 1024
S = 2048
DEPTH = 2
DFF = 2816
NCH = DFF // 128
KC = D // 128
TB = 512
NB = S // TB
ALPHA = (2.0 * DEPTH) ** 0.25
EPS = 1e-5
GROUPS = [[0, 1, 2, 3], [4, 5, 6, 7], [8, 9, 10], [11, 12, 13], [14, 15, 16, 17], [18, 19, 20, 21]]
SLOT_E = 3072
NSLOT = 8
UNITS_PER_LAYER = 2 * NCH + 5 + 3 + 3
DIAG_SLOTS = [0, 1, 2]
WIN_STREAM = [3, 7]
WOUT_SLOTS = [4, 5, 6]
NPL = 140
P_LNG, P_LNB, P_CW, P_CB, P_CG, P_CBT, P_LW, P_LB, P_BA, P_BX, P_LAM, P_SINK = (
    0, 24, 48, 110, 112, 114, 116, 124, 126, 128, 130, 132)
GELU_C = 0.7978845608028654 * 2.0
SCHED_VERBOSE = False
SIM_ONLY = False
SIM_NF32 = 0
SIM_NBF = 0
SIM_NBANK = 8
SIM_YMIX2 = 0


class Res:
    __slots__ = ("name", "w", "r", "const")

    def __init__(self, name):
        self.name = name
        self.w = None
        self.r = []
        self.const = False


class Op:
    __slots__ = ("i", "eng", "emit", "deps", "cost", "dsem", "pos", "fin", "crit", "start", "tab")

    def __init__(self, i, eng, emit, deps, cost, dsem):
        self.i, self.eng, self.emit, self.deps, self.cost, self.dsem = i, eng, emit, deps, cost, dsem
        self.pos = None
        self.fin = None


ENGS = ("pe", "act", "dve", "pool", "sp")
ACT_TABS = {"Silu": "silu", "Sigmoid": "sig", "Exp": "exp", "Sqrt": "sqrt", "Ln": "exp"}
TAB_COST = 1.3
FIFO_ENGS = ("sp",)
LN_POOL = 1
POOL_TT_COST = 1.45
POOL_DMA_ISSUE = 1.2
SCHED_WINDOW = 24
CP_PRIO = 1
SCHED_LAT = 0.25
STRICT_SAME_ENGINE = 0
DMA_RATE = 170e3
DMA_LAT = 3.0


class Sched:
    def __init__(self, nc, stack):
        self.nc = nc
        self.ops = []
        self.sem = {e: stack.enter_context(nc.semaphore("s_" + e)) for e in ENGS}
        self.dsem = {}
        self.stack = stack
        self.marks = []

    def mark(self, name):
        self.marks.append((name, len(self.ops)))

    def new_dsem(self, name):
        self.dsem[name] = self.stack.enter_context(self.nc.semaphore("d_" + name))

    def _record(self, eng, emit, reads, writes, cost, dsem):
        i = len(self.ops)
        deps = {}
        for r in reads:
            if r.w is not None:
                deps[r.w] = True
        for w in writes:
            if w.w is not None:
                deps.setdefault(w.w, False)
            for t in w.r:
                deps.setdefault(t, False)
        deps.pop(i, None)
        o_ = Op(i, eng, emit, deps, cost, dsem)
        o_.tab = None
        if eng == "act":
            for nm in emit.__code__.co_names:
                if nm in ACT_TABS:
                    o_.tab = ACT_TABS[nm]
                    break
        self.ops.append(o_)
        for r in reads:
            if not r.const:
                r.r.append(i)
        for w in writes:
            w.w = i
            w.r = []
        return i

    def op(self, eng, emit, reads=(), writes=(), cost=None, n=512):
        if cost is None:
            if eng == "dve":
                cost = 0.12 + n / 960.0
            elif eng == "act":
                cost = 0.22 + n / 1200.0
            else:
                cost = 0.5
        return self._record(eng, emit, reads, writes, cost, None)

    def dma(self, qeng, emit, dname, reads=(), writes=(), nbytes=262144):
        return self._record(qeng, emit, reads, writes, nbytes / DMA_RATE, dname)

    def schedule(self):
        ops = self.ops
        n = len(ops)
        pend = {e: [] for e in ENGS}
        for o in ops:
            pend[o.eng].append(o.i)
        head = {e: 0 for e in ENGS}
        done = [False] * n
        tfree = {e: 0.0 for e in ENGS}
        dma_free = 0.0
        order = {e: [] for e in ENGS}
        remaining = n
        cur_tab = None
        tail = [0.0] * n
        if CP_PRIO:
            dependents = [[] for _ in range(n)]
            for o in ops:
                for d in o.deps:
                    dependents[d].append(o.i)
            for i in range(n - 1, -1, -1):
                m = 0.0
                for j in dependents[i]:
                    if tail[j] > m:
                        m = tail[j]
                tail[i] = ops[i].cost + (DMA_LAT if ops[i].dsem is not None else 0.0) + m
        while remaining:
            best = None
            for e in ENGS:
                lst = pend[e]
                h = head[e]
                while h < len(lst) and done[lst[h]]:
                    h += 1
                head[e] = h
                if h >= len(lst):
                    continue
                lim = 1 if e in FIFO_ENGS else SCHED_WINDOW
                cnt = 0
                j = h
                te = tfree[e]
                while j < len(lst) and cnt < lim:
                    idx = lst[j]
                    j += 1
                    if done[idx]:
                        continue
                    cnt += 1
                    o = ops[idx]
                    ready = te
                    ok = True
                    crit = -1
                    for d, raw in o.deps.items():
                        od = ops[d]
                        f = od.fin
                        if f is None:
                            ok = False
                            break
                        if od.eng != e or od.dsem is not None or ((raw or STRICT_SAME_ENGINE) and e != "pe"):
                            f += SCHED_LAT
                        if f > ready:
                            ready = f
                            crit = d
                    if not ok:
                        continue
                    if o.tab is not None and o.tab != cur_tab:
                        ready += TAB_COST
                    o.crit = crit
                    if CP_PRIO:
                        key = (max(ready, te), -tail[idx], idx)
                        if best is None or key < best[2]:
                            best = ((ready, idx), e, key)
                    else:
                        if best is None or (ready, idx) < best[0]:
                            best = ((ready, idx), e, None)
                        if ready <= te:
                            break
            assert best is not None, "scheduler stuck"
            (ready, idx), e = best[0], best[1]
            o = ops[idx]
            if o.dsem is not None:
                tfree[e] = ready + (POOL_DMA_ISSUE if e == "pool" else 0.06)
                st_ = max(ready, dma_free)
                o.fin = st_ + o.cost + DMA_LAT
                dma_free = st_ + o.cost
            else:
                if o.tab is not None:
                    cur_tab = o.tab
                o.fin = ready + o.cost
                tfree[e] = o.fin
            o.start = ready
            if o.crit == -1 and order[e]:
                o.crit = -2 - order[e][-1]
            done[idx] = True
            order[e].append(idx)
            remaining -= 1
        self.order = order
        self.makespan = max(o.fin for o in ops)

    def _assign(self):
        ops = self.ops
        pos = [0] * len(ops)
        for e in ENGS:
            for p, idx in enumerate(self.order[e]):
                pos[idx] = p + 1
        self.waits = {}
        needed = set()
        for e in ENGS:
            waited = {}
            for idx in self.order[e]:
                o = ops[idx]
                strict = o.dsem is not None
                wl = []
                for d, raw in o.deps.items():
                    od = ops[d]
                    if od.dsem is None and od.eng == e and not strict:
                        if e == "pe" or (not raw and not STRICT_SAME_ENGINE):
                            continue
                    key = ("d_" + od.dsem) if od.dsem is not None else od.eng
                    if waited.get(key, 0) >= pos[d]:
                        continue
                    waited[key] = pos[d]
                    wl.append(d)
                    needed.add(d)
                self.waits[idx] = wl
        cnt = {e: 0 for e in ENGS}
        dcnt = {k: 0 for k in self.dsem}
        tok = [None] * len(ops)
        for e in ENGS:
            for idx in self.order[e]:
                o = ops[idx]
                if o.dsem is not None:
                    dcnt[o.dsem] += 16
                    tok[idx] = (self.dsem[o.dsem], dcnt[o.dsem])
                elif idx in needed:
                    cnt[e] += 1
                    tok[idx] = (self.sem[e], cnt[e])
        self.tok = tok
        self.dcnt_final = dcnt
        self.ninc = len(needed)

    def emit_all_one(self, e, eng):
        if not hasattr(self, "tok"):
            self._assign()
        ops = self.ops
        tok = self.tok
        for idx in self.order[e]:
            o = ops[idx]
            for d in self.waits[idx]:
                sem, val = tok[d]
                eng.wait_ge(sem, val)
            ins = o.emit(eng)
            if tok[idx] is not None:
                ins.then_inc(tok[idx][0], 16 if o.dsem is not None else 1)
        if e == "sp":
            for k, v in self.dcnt_final.items():
                if k == "out" and v > 0:
                    eng.wait_ge(self.dsem[k], v)


class Pool:
    def __init__(self, nc, stack, name, n, shape, dt):
        self.tiles = [stack.enter_context(nc.sbuf_tensor(f"{name}{i}", shape, dt)) for i in range(n)]
        self.res = [Res(f"{name}{i}") for i in range(n)]
        self.gen = [0] * n
        self.i = 0
        self.n = n

    def get(self):
        i = self.i
        self.i = (self.i + 1) % self.n
        self.gen[i] += 1
        return TRef(self, i, self.gen[i])


class TRef:
    __slots__ = ("pool", "i", "g")

    def __init__(self, pool, i, g):
        self.pool, self.i, self.g = pool, i, g

    @property
    def t(self):
        return self.pool.tiles[self.i]

    @property
    def r(self):
        assert self.pool.gen[self.i] == self.g, "stale scratch tile"
        return self.pool.res[self.i]


def build_program(n_layers, stop_after=None, skip=()):
    nc = bass.Bass("TRN2", target_bir_lowering=False)
    NU = n_layers * UNITS_PER_LAYER
    xin = nc.dram_tensor("xin", [D, S], F32, kind="ExternalInput").ap()
    wst = nc.dram_tensor("wst", [NU, 128, SLOT_E], F32, kind="ExternalInput").ap()
    prm = nc.dram_tensor("prm", [128, n_layers * NPL], F32, kind="ExternalInput").ap()
    bdd = nc.dram_tensor("bdd", [128, n_layers * 512], F32, kind="ExternalInput").ap()
    bias_d = nc.dram_tensor("biasT", [128, 2048], F32, kind="ExternalInput").ap()
    mask_d = nc.dram_tensor("mask", [128, 256], F32, kind="ExternalInput").ap()
    ident_d = nc.dram_tensor("ident", [128, 128], F32, kind="ExternalInput").ap()
    yout = nc.dram_tensor("yout", [D, S], F32, kind="ExternalOutput").ap()
    ydbg = nc.dram_tensor("ydbg", [D, S], F32, kind="ExternalOutput").ap() if stop_after == "mixer" else None

    with contextlib.ExitStack() as st:
        sc = Sched(nc, st)
        for i in range(NSLOT):
            sc.new_dsem(f"slot{i}")
        sc.new_dsem("const")
        for k in range(KC):
            sc.new_dsem(f"xin{k}")
        sc.new_dsem("out")
        for l in range(n_layers):
            sc.new_dsem(f"bd{l}")

        def sb(name, shape, dt):
            return st.enter_context(nc.sbuf_tensor(name, shape, dt))

        xr = sb("xr", [128, KC, S if not SIM_NF32 else S // 4], F32)
        xb = sb("xb", [128, KC, S], BF16)
        ring = sb("ring", [128, NSLOT, SLOT_E], BF16)
        hy = sb("hy", [128, 8, TB], BF16)
        params = sb("params", [128, n_layers * NPL], F32)
        dparams = sb("dparams", [128, n_layers * 64], F32)
        bdb = sb("bdb", [128, n_layers * 512], BF16)
        EB = sb("EB", [128, 2048], BF16)
        maskt = sb("maskt", [128, 256], F32)
        ident = sb("ident_s", [128, 128], F32)
        ones = sb("ones_s", [128, 128], F32R)
        ones_f = sb("ones_f", [128, 128], F32)
        identb = sb("identb", [128, 128], BF16)
        qsb = sb("qsb", [128, 4, TB], BF16)
        ksb = sb("ksb", [128, S], BF16)
        vsb = sb("vsb", [128, 16, 2, 65], BF16)
        ybuf = sb("ybuf", [128, 2, 30 + TB], BF16)
        xbuf = sb("xbuf", [128, 2, 3 + TB], BF16)
        hlast = sb("hlast", [128, 2], F32)
        rden = sb("rden", [128, 8], F32)
        banks = [st.enter_context(nc.psum_tensor(f"bank{i}", [128, 512], F32)) for i in range(8)]
        if SIM_NBANK > 8:
            banks = [banks[i % 8] for i in range(SIM_NBANK)]

        nbf = SIM_NBF or 4
        pb16 = Pool(nc, st, "tb", nbf, [128, TB], BF16)
        rdp = Pool(nc, st, "rd", 4, [128, 8], F32)
        pfr = Pool(nc, st, "tr", 3, [128, TB], F32R)
        nfree = nc.sbuf_bytes_remaining
        nf32 = SIM_NF32 or min(14, (nfree - 1024) // 2048)
        assert nf32 >= 7, f"not enough SBUF for scratch tiles: {nf32}"
        pf32 = Pool(nc, st, "tf", nf32, [128, TB], F32)

        R_xr = [[Res(f"xr{k}_{s}") for s in range(NB)] for k in range(KC)]
        R_xb = [[Res(f"xb{k}_{s}") for s in range(NB)] for k in range(KC)]
        R_slot = [Res(f"slot{i}") for i in range(NSLOT)]
        R_hy = [Res(f"hy{i}") for i in range(8)]
        R_bank = [Res(f"bank{i}") for i in range(SIM_NBANK)]
        R_ym2 = [[Res(f"ym{p}_{i}") for i in range(8)] for p in range(2)]

        def RY(s, c):
            return R_ym2[s % 2][c] if SIM_YMIX2 else R_hy[c]
        R_const = Res("const")
        R_dpar = Res("dpar")
        R_qsb = Res("qsb")
        R_ksb = [Res(f"ksb{s}") for s in range(NB)]
        R_vsb = [Res(f"vsb{s}") for s in range(NB)]
        R_ybuf = [Res("ybuf0"), Res("ybuf1")]
        R_xbuf = [Res("xbuf0"), Res("xbuf1")]
        R_hlast = [Res("hlast0"), Res("hlast1")]
        R_rden = Res("rden")
        R_out = Res("out")
        bank_i = [0]

        def nbank():
            i = bank_i[0]
            bank_i[0] = (i + 1) % SIM_NBANK
            return i

        def blk(s):
            return slice(s * TB, (s + 1) * TB)

        xin_v = xin.rearrange("(k p) t -> p k t", p=128)
        for s in range(NB):
            sc.dma("sp", lambda e, s=s: e.dma_start(out=xr[:, :, blk(s)], in_=xin_v[:, :, blk(s)]),
                   f"xin{s}", writes=[R_xr[k][s] for k in range(KC)], nbytes=128 * KC * TB * 4)

        stg0 = pf32.get()
        stg1 = pf32.get()
        sc.dma("sp", lambda e: e.dma_start(out=params[:], in_=prm[:, :]), "const", writes=[R_const])
        ebst = [pf32.get() for _ in range(4)]
        for qi_ in range(4):
            sc.new_dsem(f"eb{qi_}")
            sc.dma("sp", lambda e, qi_=qi_: e.dma_start(out=ebst[qi_].t[:], in_=bias_d[:, qi_ * 512:(qi_ + 1) * 512]),
                   f"eb{qi_}", writes=[ebst[qi_].r])
        sc.dma("sp", lambda e: e.dma_start(out=maskt[:], in_=mask_d[:, :]), "const", writes=[R_const])
        for l in range(n_layers):
            tl = stg0 if l == 0 else stg1
            sc.dma("sp", lambda e, tl=tl, l=l: e.dma_start(out=tl.t[:], in_=bdd[:, l * 512:(l + 1) * 512]),
                   f"bd{l}", writes=[tl.r])
        def load_unit(unit, slot):
            def emit(e, unit=unit, slot=slot):
                return e.dma_start(
                    out=ring[:, slot, :].rearrange("p (a b) -> p a b", a=2),
                    in_=wst[unit].rearrange("p (a b) -> p a b", a=2))
            sc.dma("pool", emit, f"slot{slot}", writes=[R_slot[slot]], nbytes=128 * SLOT_E * 4)

        def load_ffn_group(l, f, g):
            for i, j in enumerate(GROUPS[g]):
                load_unit(l * UNITS_PER_LAYER + f * NCH + j, (g % 2) * 4 + i)

        win_loaded = {}
        win_n = [0]

        def ensure_win(l, s, p):
            if s >= NB or p >= 5:
                return
            if (l, s, p) in win_loaded:
                return
            slot = WIN_STREAM[win_n[0] % 2]
            win_n[0] += 1
            win_loaded[(l, s, p)] = slot
            load_unit(l * UNITS_PER_LAYER + 2 * NCH + p, slot)

        def load_diag(l):
            for p in range(3):
                load_unit(l * UNITS_PER_LAYER + 2 * NCH + 8 + p, DIAG_SLOTS[p])

        def load_wout(l):
            for p in range(3):
                load_unit(l * UNITS_PER_LAYER + 2 * NCH + 5 + p, WOUT_SLOTS[p])

        load_ffn_group(0, 0, 0)
        load_ffn_group(0, 0, 1)

        sc.op("dve", lambda e: e.memset(ones_f[:], 1.0), writes=[R_const])
        sc.op("dve", lambda e: e.tensor_copy(ones[:], ones_f[:]), reads=[R_const], writes=[R_const])
        sc.new_dsem("ident")
        sc.dma("sp", lambda e: e.dma_start(out=ident[:], in_=ident_d[:, :]), "ident", reads=[R_const], writes=[R_const])
        sc.op("act", lambda e: e.activation(identb[:], ident[:], AF.Identity), reads=[R_const], writes=[R_const])
        sc.op("dve", lambda e: e.memset(vsb[:, :, :, 64:65], 1.0), writes=R_vsb)
        for qi_ in range(4):
            sc.op("act", lambda e, qi_=qi_: e.activation(ebst[qi_].t[:], ebst[qi_].t[:], AF.Exp),
                  reads=[ebst[qi_].r], writes=[ebst[qi_].r])
            for h2 in range(2):
                sc.op("dve", lambda e, qi_=qi_, h2=h2: e.tensor_tensor(
                    EB[:, qi_ * 512 + h2 * 256:qi_ * 512 + (h2 + 1) * 256],
                    ebst[qi_].t[:, h2 * 256:(h2 + 1) * 256], maskt[:], ALU.mult),
                    reads=[ebst[qi_].r, R_const], writes=[R_const])
        for l in range(n_layers):
            tl = stg0 if l == 0 else stg1
            sc.op("act", lambda e, tl=tl, l=l: e.activation(bdb[:, l * 512:(l + 1) * 512], tl.t[:], AF.Identity),
                  reads=[tl.r], writes=[R_const])
        for l in range(n_layers):
            po = l * NPL
            do = l * 64
            sc.op("dve", lambda e, po=po, do=do: e.tensor_scalar(
                dparams[:, do:do + 48], params[:, po:po + 48], float(ALPHA), None, ALU.mult),
                reads=[R_const], writes=[R_dpar])
            sc.op("act", lambda e, po=po, do=do: e.activation(
                dparams[:, do + 48:do + 50], params[:, po + P_LAM:po + P_LAM + 2], AF.Exp, scale=-1.0),
                reads=[R_const], writes=[R_dpar])
            sc.op("act", lambda e, do=do: e.activation(
                dparams[:, do + 48:do + 50], dparams[:, do + 48:do + 50], AF.Ln, bias=1.0),
                reads=[R_dpar], writes=[R_dpar])
            sc.op("dve", lambda e, do=do: e.tensor_scalar(
                dparams[:, do + 50:do + 52], dparams[:, do + 48:do + 50], -16.0, None, ALU.mult),
                reads=[R_dpar], writes=[R_dpar])
            sc.op("dve", lambda e, do=do: e.tensor_scalar(
                dparams[:, do + 48:do + 50], dparams[:, do + 48:do + 50], -8.0, None, ALU.mult),
                reads=[R_dpar], writes=[R_dpar])
            sc.op("act", lambda e, po=po, do=do: e.activation(
                dparams[:, do + 52:do + 60], params[:, po + P_SINK:po + P_SINK + 8], AF.Exp),
                reads=[R_const], writes=[R_dpar])
        for k in range(KC):
            for s in range(NB):
                sc.op("act", lambda e, k=k, s=s: e.activation(xb[:, k, blk(s)], xr[:, k, blk(s)], AF.Identity),
                      reads=[R_xr[k][s]], writes=[R_xb[k][s]])
                sc.op("dve", lambda e, k=k, s=s: e.tensor_scalar(
                    xr[:, k, blk(s)], xr[:, k, blk(s)], float(ALPHA), None, ALU.mult),
                    reads=[R_xr[k][s]], writes=[R_xr[k][s]])

        R_const.const = True
        R_dpar.const = True

        def ln_block(l, i, s, final):
            po = l * NPL
            do = l * 64
            b1 = nbank()
            b2 = nbank()
            for k in range(KC):
                xc = pfr.get()
                sc.op("act", lambda e, xc=xc, k=k: e.activation(xc.t[:], xr[:, k, blk(s)], AF.Identity),
                      reads=[R_xr[k][s]], writes=[xc.r])
                sc.op("pe", lambda e, xc=xc, k=k: e.matmul(banks[b1][:], ones[:], xc.t[:],
                                                            start=(k == 0), stop=(k == KC - 1)),
                      reads=[xc.r, R_const], writes=[R_bank[b1]], cost=0.23)
                sq = pfr.get()
                sc.op("act", lambda e, sq=sq, k=k: e.activation(sq.t[:], xr[:, k, blk(s)], AF.Square),
                      reads=[R_xr[k][s]], writes=[sq.r])
                sc.op("pe", lambda e, sq=sq, k=k: e.matmul(banks[b2][:], ones[:], sq.t[:],
                                                            start=(k == 0), stop=(k == KC - 1)),
                      reads=[sq.r, R_const], writes=[R_bank[b2]], cost=0.23)
            mean = pf32.get()
            sc.op("dve", lambda e: e.tensor_scalar(mean.t[:], banks[b1][:], 1.0 / D, None, ALU.mult),
                  reads=[R_bank[b1]], writes=[mean.r])
            msq = pf32.get()
            sc.op("dve", lambda e: e.tensor_tensor(msq.t[:], mean.t[:], mean.t[:], ALU.mult),
                  reads=[mean.r], writes=[msq.r])
            sc.op("dve", lambda e: e.scalar_tensor_tensor(msq.t[:], banks[b2][:], 1.0 / D, msq.t[:],
                                                          ALU.mult, ALU.subtract),
                  reads=[R_bank[b2], msq.r], writes=[msq.r])
            sc.op("act", lambda e: e.activation(msq.t[:], msq.t[:], AF.Ln, bias=EPS),
                  reads=[msq.r], writes=[msq.r])
            sc.op("act", lambda e: e.activation(msq.t[:], msq.t[:], AF.Exp, scale=-0.5),
                  reads=[msq.r], writes=[msq.r])
            rstd = msq
            sc.op("dve", lambda e: e.tensor_tensor(mean.t[:], mean.t[:], rstd.t[:], ALU.mult),
                  reads=[mean.r, rstd.r], writes=[mean.r])
            mrs = mean
            for k in range(KC):
                if LN_POOL:
                    sc.op("pool", lambda e, k=k: e.tensor_tensor(xr[:, k, blk(s)], xr[:, k, blk(s)], rstd.t[:], ALU.mult),
                          reads=[R_xr[k][s], rstd.r], writes=[R_xr[k][s]], cost=POOL_TT_COST)
                else:
                    sc.op("dve", lambda e, k=k: e.tensor_tensor(xr[:, k, blk(s)], xr[:, k, blk(s)], rstd.t[:], ALU.mult),
                          reads=[R_xr[k][s], rstd.r], writes=[R_xr[k][s]])
                sc.op("dve", lambda e, k=k: e.tensor_tensor(xr[:, k, blk(s)], xr[:, k, blk(s)], mrs.t[:], ALU.subtract),
                      reads=[R_xr[k][s], mrs.r], writes=[R_xr[k][s]])
                gc = po + P_LNG + i * 8 + k
                bc = po + P_LNB + i * 8 + k
                if final:
                    sc.op("act", lambda e, k=k, gc=gc, bc=bc: e.activation(
                        xr[:, k, blk(s)], xr[:, k, blk(s)], AF.Identity, bias=params[:, bc:bc + 1],
                        scale=params[:, gc:gc + 1]),
                        reads=[R_xr[k][s], R_const], writes=[R_xr[k][s]])
                    sc.dma("sp", lambda e, k=k: e.dma_start(out=yout[k * 128:(k + 1) * 128, blk(s)],
                                                            in_=xr[:, k, blk(s)]),
                           "out", reads=[R_xr[k][s]], writes=[R_out])
                else:
                    ag = do + i * 8 + k
                    ab = do + 24 + i * 8 + k
                    sc.op("act", lambda e, k=k, gc=gc, bc=bc: e.activation(
                        xb[:, k, blk(s)], xr[:, k, blk(s)], AF.Identity, bias=params[:, bc:bc + 1],
                        scale=params[:, gc:gc + 1]),
                        reads=[R_xr[k][s], R_const], writes=[R_xb[k][s]])
                    sc.op("dve", lambda e, k=k, ag=ag, ab=ab: e.tensor_scalar(
                        xr[:, k, blk(s)], xr[:, k, blk(s)], dparams[:, ag:ag + 1], dparams[:, ab:ab + 1],
                        ALU.mult, ALU.add),
                        reads=[R_xr[k][s], R_dpar], writes=[R_xr[k][s]])

        def _mm_group(e, lst):
            ins = None
            n = len(lst)
            for i, (o, a, b) in enumerate(lst):
                ins = e.matmul(o, a, b, start=(i == 0), stop=(i == n - 1))
            return ins

        def ffn_up(g, s, hb):
            for jj, j in enumerate(GROUPS[g]):
                slot = (g % 2) * 4 + jj
                bg = nbank()
                sc.op("pe", lambda e, slot=slot, bg=bg: _mm_group(e, [
                    (banks[bg][:], ring[:, slot, k * 128:(k + 1) * 128], xb[:, k, blk(s)]) for k in range(KC)]),
                    reads=[R_slot[slot]] + [R_xb[k][s] for k in range(KC)], writes=[R_bank[bg]], cost=1.78)
                bu = nbank()
                sc.op("pe", lambda e, slot=slot, bu=bu: _mm_group(e, [
                    (banks[bu][:], ring[:, slot, 1024 + k * 128:1024 + (k + 1) * 128], xb[:, k, blk(s)])
                    for k in range(KC)]),
                    reads=[R_slot[slot]] + [R_xb[k][s] for k in range(KC)], writes=[R_bank[bu]], cost=1.78)
                sg = pb16.get()
                sc.op("act", lambda e, sg=sg, bg=bg: e.activation(sg.t[:], banks[bg][:], AF.Silu),
                      reads=[R_bank[bg]], writes=[sg.r])
                sc.op("dve", lambda e, sg=sg, bu=bu, jj=jj: e.scalar_tensor_tensor(
                    hy[:, hb * 4 + jj, :], sg.t[:], 0.5, banks[bu][:], ALU.mult, ALU.mult),
                    reads=[sg.r, R_bank[bu]], writes=[R_hy[hb * 4 + jj]])

        def ffn_down(g, s, hb):
            ng = len(GROUPS[g])
            for m in range(KC):
                by = nbank()
                sc.op("pe", lambda e, m=m, by=by: _mm_group(e, [
                    (banks[by][:], ring[:, (g % 2) * 4 + jj, 2048 + m * 128:2048 + (m + 1) * 128],
                     hy[:, hb * 4 + jj, :]) for jj in range(ng)]),
                    reads=[R_slot[(g % 2) * 4 + jj] for jj in range(ng)] + [R_hy[hb * 4 + jj] for jj in range(ng)],
                    writes=[R_bank[by]], cost=0.2225 * ng)
                sc.op("dve", lambda e, m=m, by=by: e.tensor_tensor(
                    xr[:, m, blk(s)], banks[by][:], xr[:, m, blk(s)], ALU.add),
                    reads=[R_bank[by], R_xr[m][s]], writes=[R_xr[m][s]])

        hcount = [0]

        cur_l = [0]

        def u_chunk_ap(cc, k, s):
            p, loc = divmod(cc, 3)
            return win_loaded[(cur_l[0], s, p)], loc * 1024 + k * 128

        def u_prefetch(cc, s):
            p = cc // 3
            ensure_win(cur_l[0], s, p)
            if p + 1 < 5:
                ensure_win(cur_l[0], s, p + 1)
            else:
                ensure_win(cur_l[0], s + 1, 0)

        def u_matmul(cc, s):
            u_prefetch(cc, s)
            b = nbank()
            slot0, _ = u_chunk_ap(cc, 0, s)

            def emit(e, cc=cc, b=b):
                lst = []
                for k in range(KC):
                    slot, off = u_chunk_ap(cc, k, s)
                    lst.append((banks[b][:], ring[:, slot, off:off + 128], xb[:, k, blk(s)]))
                return _mm_group(e, lst)
            sc.op("pe", emit, reads=[R_slot[slot0]] + [R_xb[k][s] for k in range(KC)], writes=[R_bank[b]],
                  cost=1.8)
            return b

        def mixer_block(l, s):
            cur_l[0] = l
            po = l * NPL
            do = l * 64
            bo = l * 512
            if 'conv' not in skip:
              _conv_branch(l, s, po, do, bo)
            if 'lru' not in skip:
              _lru_branch(l, s, po, do, bo)
            if 'attn' not in skip:
              _attn_branch(l, s, po, do, bo)

        def _conv_branch(l, s, po, do, bo):
            ba = [u_matmul(0, s), u_matmul(1, s)]
            bg = [u_matmul(2, s), u_matmul(3, s)]
            accs = []
            for c in range(2):
                sig = pf32.get()
                sc.op("act", lambda e, sig=sig, c=c: e.activation(sig.t[:], banks[bg[c]][:], AF.Sigmoid),
                      reads=[R_bank[bg[c]]], writes=[sig.r])
                sc.op("dve", lambda e, sig=sig, c=c: e.tensor_tensor(
                    ybuf[:, c, 30:30 + TB], banks[ba[c]][:], sig.t[:], ALU.mult),
                    reads=[R_bank[ba[c]], sig.r], writes=[R_ybuf[c]])
                acc = pf32.get()
                cb = po + P_CB + c
                bc = nbank()

                def emit_conv(e, c=c, bc=bc):
                    ins = None
                    for j in range(31):
                        i = c * 31 + j
                        ins = e.matmul(banks[bc][:], ring[:, DIAG_SLOTS[i // 24], (i % 24) * 128:(i % 24 + 1) * 128],
                                       ybuf[:, c, j:j + TB], start=(j == 0), stop=(j == 30))
                    return ins
                sc.op("pe", emit_conv, reads=[R_slot[x] for x in DIAG_SLOTS] + [R_ybuf[c]], writes=[R_bank[bc]],
                      cost=31 * 0.222)
                sc.op("act", lambda e, acc=acc, bc=bc, cb=cb: e.activation(
                    acc.t[:], banks[bc][:], AF.Identity, bias=params[:, cb:cb + 1]),
                    reads=[R_bank[bc], R_const], writes=[acc.r])
                sc.op("dve", lambda e, c=c: e.tensor_copy(ybuf[:, c, 0:30], ybuf[:, c, TB:TB + 30]),
                      reads=[R_ybuf[c]], writes=[R_ybuf[c]], n=30)
                accs.append(acc)
            b1 = nbank()
            b2 = nbank()
            for c in range(2):
                ar = pfr.get()
                sc.op("act", lambda e, ar=ar, c=c: e.activation(ar.t[:], accs[c].t[:], AF.Identity),
                      reads=[accs[c].r], writes=[ar.r])
                sc.op("pe", lambda e, ar=ar, c=c: e.matmul(banks[b1][:], ones[:], ar.t[:], start=(c == 0), stop=(c == 1)),
                      reads=[ar.r, R_const], writes=[R_bank[b1]], cost=0.23)
                sq = pfr.get()
                sc.op("act", lambda e, sq=sq, c=c: e.activation(sq.t[:], accs[c].t[:], AF.Square),
                      reads=[accs[c].r], writes=[sq.r])
                sc.op("pe", lambda e, sq=sq, c=c: e.matmul(banks[b2][:], ones[:], sq.t[:], start=(c == 0), stop=(c == 1)),
                      reads=[sq.r, R_const], writes=[R_bank[b2]], cost=0.23)
            mean2 = pf32.get()
            sc.op("dve", lambda e: e.tensor_scalar(mean2.t[:], banks[b1][:], 1.0 / 256, None, ALU.mult),
                  reads=[R_bank[b1]], writes=[mean2.r])
            var = pf32.get()
            sc.op("dve", lambda e: e.tensor_tensor(var.t[:], mean2.t[:], mean2.t[:], ALU.mult),
                  reads=[mean2.r], writes=[var.r])
            sc.op("dve", lambda e: e.scalar_tensor_tensor(var.t[:], banks[b2][:], 1.0 / 256, var.t[:],
                                                          ALU.mult, ALU.subtract),
                  reads=[R_bank[b2], var.r], writes=[var.r])
            sc.op("act", lambda e: e.activation(var.t[:], var.t[:], AF.Ln, bias=EPS),
                  reads=[var.r], writes=[var.r])
            sc.op("act", lambda e: e.activation(var.t[:], var.t[:], AF.Exp, scale=-0.5),
                  reads=[var.r], writes=[var.r])
            sc.op("dve", lambda e: e.tensor_tensor(mean2.t[:], mean2.t[:], var.t[:], ALU.mult),
                  reads=[mean2.r, var.r], writes=[mean2.r])
            for c in range(2):
                acc = accs[c]
                sc.op("dve", lambda e, acc=acc: e.tensor_tensor(acc.t[:], acc.t[:], var.t[:], ALU.mult),
                      reads=[acc.r, var.r], writes=[acc.r])
                sc.op("dve", lambda e, acc=acc: e.tensor_tensor(acc.t[:], acc.t[:], mean2.t[:], ALU.subtract),
                      reads=[acc.r, mean2.r], writes=[acc.r])
                cg = po + P_CG + c
                cbt = po + P_CBT + c
                sc.op("act", lambda e, acc=acc, c=c, cg=cg, cbt=cbt: e.activation(
                    hy[:, c, :], acc.t[:], AF.Silu, bias=params[:, cbt:cbt + 1], scale=params[:, cg:cg + 1]),
                    reads=[acc.r, R_const], writes=[RY(s, c)])


        def _lru_branch(l, s, po, do, bo):
            bx = [u_matmul(4, s), u_matmul(5, s)]
            bgb = [u_matmul(6, s), u_matmul(7, s)]
            for c in range(2):
                sc.op("act", lambda e, c=c: e.activation(xbuf[:, c, 3:3 + TB], banks[bx[c]][:], AF.Identity),
                      reads=[R_bank[bx[c]]], writes=[R_xbuf[c]])
                xc = pf32.get()
                lb = po + P_LB + c
                b4 = nbank()

                def emit_c4(e, c=c, b4=b4):
                    ins = None
                    for j in range(4):
                        i = 62 + c * 4 + j
                        ins = e.matmul(banks[b4][:], ring[:, DIAG_SLOTS[i // 24], (i % 24) * 128:(i % 24 + 1) * 128],
                                       xbuf[:, c, j:j + TB], start=(j == 0), stop=(j == 3))
                    return ins
                sc.op("pe", emit_c4, reads=[R_slot[x] for x in DIAG_SLOTS] + [R_xbuf[c]], writes=[R_bank[b4]],
                      cost=4 * 0.222)
                sc.op("act", lambda e, xc=xc, b4=b4, lb=lb: e.activation(
                    xc.t[:], banks[b4][:], AF.Identity, bias=params[:, lb:lb + 1]),
                    reads=[R_bank[b4], R_const], writes=[xc.r])
                sc.op("dve", lambda e, c=c: e.tensor_copy(xbuf[:, c, 0:3], xbuf[:, c, TB:TB + 3]),
                      reads=[R_xbuf[c]], writes=[R_xbuf[c]], n=3)
                xcb = pb16.get()
                sc.op("act", lambda e, xc=xc, xcb=xcb: e.activation(xcb.t[:], xc.t[:], AF.Identity),
                      reads=[xc.r], writes=[xcb.r])
                bra = nbank()
                sc.op("pe", lambda e, xcb=xcb, bra=bra, c=c: e.matmul(
                    banks[bra][:], bdb[:, bo + c * 128:bo + (c + 1) * 128], xcb.t[:], start=True, stop=True),
                    reads=[xcb.r, R_const], writes=[R_bank[bra]], cost=0.25)
                brx = nbank()
                sc.op("pe", lambda e, xcb=xcb, brx=brx, c=c: e.matmul(
                    banks[brx][:], bdb[:, bo + 256 + c * 128:bo + 256 + (c + 1) * 128], xcb.t[:],
                    start=True, stop=True),
                    reads=[xcb.r, R_const], writes=[R_bank[brx]], cost=0.25)
                rr = pf32.get()
                pba = po + P_BA + c
                pbx = po + P_BX + c
                sc.op("act", lambda e, rr=rr, bra=bra, pba=pba: e.activation(
                    rr.t[:], banks[bra][:], AF.Sigmoid, bias=params[:, pba:pba + 1]),
                    reads=[R_bank[bra], R_const], writes=[rr.r])
                ii = pf32.get()
                sc.op("act", lambda e, ii=ii, brx=brx, pbx=pbx: e.activation(
                    ii.t[:], banks[brx][:], AF.Sigmoid, bias=params[:, pbx:pbx + 1]),
                    reads=[R_bank[brx], R_const], writes=[ii.r])
                aa = pf32.get()
                cl = do + 48 + c
                cl2 = do + 50 + c
                sc.op("act", lambda e, aa=aa, rr=rr, cl=cl: e.activation(
                    aa.t[:], rr.t[:], AF.Exp, scale=dparams[:, cl:cl + 1]),
                    reads=[rr.r, R_dpar], writes=[aa.r])
                sc.op("act", lambda e, rr=rr, cl2=cl2: e.activation(
                    rr.t[:], rr.t[:], AF.Exp, scale=dparams[:, cl2:cl2 + 1]),
                    reads=[rr.r, R_dpar], writes=[rr.r])
                sc.op("dve", lambda e, rr=rr: e.tensor_scalar(rr.t[:], rr.t[:], -1.0, 1.0, ALU.mult, ALU.add),
                      reads=[rr.r], writes=[rr.r])
                sc.op("act", lambda e, rr=rr: e.activation(rr.t[:], rr.t[:], AF.Sqrt),
                      reads=[rr.r], writes=[rr.r])
                sc.op("dve", lambda e, ii=ii, xc=xc: e.tensor_tensor(ii.t[:], ii.t[:], xc.t[:], ALU.mult),
                      reads=[ii.r, xc.r], writes=[ii.r])
                sc.op("dve", lambda e, ii=ii, rr=rr: e.tensor_tensor(ii.t[:], ii.t[:], rr.t[:], ALU.mult),
                      reads=[ii.r, rr.r], writes=[ii.r])
                hh = pf32.get()
                if s == 0:
                    sc.op("dve", lambda e, hh=hh, aa=aa, ii=ii: e.tensor_tensor_scan(
                        hh.t[:], aa.t[:], ii.t[:], 0.0, ALU.mult, ALU.add),
                        reads=[aa.r, ii.r], writes=[hh.r])
                else:
                    sc.op("dve", lambda e, hh=hh, aa=aa, ii=ii, c=c: e.tensor_tensor_scan(
                        hh.t[:], aa.t[:], ii.t[:], hlast[:, c:c + 1], ALU.mult, ALU.add),
                        reads=[aa.r, ii.r, R_hlast[c]], writes=[hh.r])
                sc.op("dve", lambda e, hh=hh, c=c: e.tensor_copy(hlast[:, c:c + 1], hh.t[:, TB - 1:TB]),
                      reads=[hh.r], writes=[R_hlast[c]], n=1)
                gsb = aa
                gs = pf32.get()
                sc.op("act", lambda e, gs=gs, c=c: e.activation(gs.t[:], banks[bgb[c]][:], AF.Identity),
                      reads=[R_bank[bgb[c]]], writes=[gs.r])
                t2 = pf32.get()
                sc.op("dve", lambda e, t2=t2, gs=gs: e.tensor_tensor(t2.t[:], gs.t[:], gs.t[:], ALU.mult),
                      reads=[gs.r], writes=[t2.r])
                sc.op("dve", lambda e, t2=t2: e.tensor_scalar(t2.t[:], t2.t[:], 0.044715, 1.0, ALU.mult, ALU.add),
                      reads=[t2.r], writes=[t2.r])
                sc.op("dve", lambda e, t2=t2, gs=gs: e.tensor_tensor(t2.t[:], t2.t[:], gs.t[:], ALU.mult),
                      reads=[t2.r, gs.r], writes=[t2.r])
                sc.op("act", lambda e, t2=t2: e.activation(t2.t[:], t2.t[:], AF.Sigmoid, scale=GELU_C),
                      reads=[t2.r], writes=[t2.r])
                sc.op("dve", lambda e, t2=t2, gs=gs: e.tensor_tensor(t2.t[:], t2.t[:], gs.t[:], ALU.mult),
                      reads=[t2.r, gs.r], writes=[t2.r])
                sc.op("dve", lambda e, t2=t2, hh=hh, c=c: e.tensor_tensor(hy[:, 2 + c, :], t2.t[:], hh.t[:], ALU.mult),
                      reads=[t2.r, hh.r], writes=[RY(s, 2 + c)])


        def _attn_branch(l, s, po, do, bo):
            for c in range(4):
                b = u_matmul(8 + c, s)
                sc.op("act", lambda e, b=b, c=c: e.activation(qsb[:, c, :], banks[b][:], AF.Identity, scale=0.125),
                      reads=[R_bank[b]], writes=[R_qsb])
            b = u_matmul(12, s)
            sc.op("act", lambda e, b=b: e.activation(ksb[:, blk(s)], banks[b][:], AF.Identity),
                  reads=[R_bank[b]], writes=[R_ksb[s]])
            bv = nbank()
            u_prefetch(13, s)
            vslot, voff = u_chunk_ap(13, 0, s)

            def emit_v(e, bv=bv):
                ins = None
                for tt in range(4):
                    T = s * 4 + tt
                    for k in range(KC):
                        slot, off = u_chunk_ap(13, k, s)
                        ins = e.matmul(banks[bv][:, tt * 128:(tt + 1) * 128], xb[:, k, T * 128:(T + 1) * 128],
                                       ring[:, slot, off:off + 128], start=(k == 0), stop=(k == KC - 1))
                return ins
            sc.op("pe", emit_v, reads=[R_slot[vslot]] + [R_xb[k][s] for k in range(KC)], writes=[R_bank[bv]], cost=2.2)
            sc.op("dve", lambda e, bv=bv: e.tensor_copy(
                vsb[:, s * 4:(s + 1) * 4, :, 0:64],
                banks[bv][:].rearrange("p (t g d) -> p t g d", t=4, g=2)),
                reads=[R_bank[bv]], writes=[R_vsb[s]])

            for tt in range(4):
                T = s * 4 + tt
                first = (T == 0)
                ob = [nbank(), nbank()]
                kprev_res = R_ksb[(T - 1) // 4] if not first else None
                vprev_res = R_vsb[(T - 1) // 4] if not first else None
                for c0 in (0, 2):
                    sbk = [nbank(), nbank()]

                    def emit_s(e, sbk=sbk, c0=c0, T=T, first=first, tt=tt):
                        ins = None
                        for ci in range(2):
                            for hh_ in range(2):
                                pr = slice(hh_ * 64, (hh_ + 1) * 64)
                                if not first:
                                    ins = e.matmul(banks[sbk[hh_]][:, ci * 256:ci * 256 + 128],
                                                   ksb[pr, (T - 1) * 128:T * 128],
                                                   qsb[pr, c0 + ci, tt * 128:(tt + 1) * 128], start=True, stop=True)
                                ins = e.matmul(banks[sbk[hh_]][:, ci * 256 + 128:ci * 256 + 256],
                                               ksb[pr, T * 128:(T + 1) * 128],
                                               qsb[pr, c0 + ci, tt * 128:(tt + 1) * 128], start=True, stop=True)
                        return ins
                    rds = [R_qsb, R_ksb[s]] + ([kprev_res] if kprev_res is not None else [])
                    sc.op("pe", emit_s, reads=rds, writes=[R_bank[sbk[0]], R_bank[sbk[1]]], cost=0.6)
                    pts = []
                    for hh_ in range(2):
                        ex = pf32.get()
                        pt = pb16.get()
                        eo = hh_ * 1024 + c0 * 256
                        if first:
                            sc.op("act", lambda e, ex=ex, hh_=hh_, sbk=sbk: e.activation(
                                ex.t[:].rearrange("p (h a q) -> p h a q", h=2, a=2)[:, :, 1, :],
                                banks[sbk[hh_]][:].rearrange("p (h a q) -> p h a q", h=2, a=2)[:, :, 1, :], AF.Exp),
                                reads=[R_bank[sbk[hh_]]], writes=[ex.r])
                            sc.op("dve", lambda e, ex=ex, pt=pt, eo=eo: e.tensor_tensor(
                                pt.t[:].rearrange("p (h a q) -> p h a q", h=2, a=2)[:, :, 1, :],
                                ex.t[:].rearrange("p (h a q) -> p h a q", h=2, a=2)[:, :, 1, :],
                                EB[:, eo:eo + 512].rearrange("p (h a q) -> p h a q", h=2, a=2)[:, :, 1, :],
                                ALU.mult),
                                reads=[ex.r, R_const], writes=[pt.r])
                        else:
                            sc.op("act", lambda e, ex=ex, hh_=hh_, sbk=sbk: e.activation(
                                ex.t[:], banks[sbk[hh_]][:], AF.Exp),
                                reads=[R_bank[sbk[hh_]]], writes=[ex.r])
                            sc.op("dve", lambda e, ex=ex, pt=pt, eo=eo: e.tensor_tensor(
                                pt.t[:], ex.t[:], EB[:, eo:eo + 512], ALU.mult),
                                reads=[ex.r, R_const], writes=[pt.r])
                        pts.append(pt)

                    def emit_o(e, pts=pts, c0=c0, T=T, first=first, ob=ob):
                        ins = None
                        for ci in range(2):
                            for hh_ in range(2):
                                o_ap = banks[ob[hh_]][:, (c0 + ci) * 65:(c0 + ci + 1) * 65]
                                pt = pts[hh_]
                                if not first:
                                    e.matmul(o_ap, pt.t[:, ci * 256:ci * 256 + 128], vsb[:, T - 1, hh_, :],
                                             start=True, stop=False)
                                ins = e.matmul(o_ap, pt.t[:, ci * 256 + 128:ci * 256 + 256], vsb[:, T, hh_, :],
                                               start=first, stop=True)
                        return ins
                    rds = [pts[0].r, pts[1].r, R_vsb[s]] + ([vprev_res] if vprev_res is not None else [])
                    sc.op("pe", emit_o, reads=rds, writes=[R_bank[ob[0]], R_bank[ob[1]]], cost=0.45)
                otok = pb16.get()
                rd = rdp.get()
                ovs = []
                for hh_ in range(2):
                    ov = banks[ob[hh_]][:, 0:260].rearrange("p (c d) -> p c d", c=4)
                    ovs.append(ov)
                    sk = do + 52 + hh_ * 4
                    sc.op("dve", lambda e, ov=ov, hh_=hh_, sk=sk, rd=rd: e.tensor_tensor(
                        rd.t[:, hh_ * 4:(hh_ + 1) * 4].rearrange("p (c o) -> p c o", o=1), ov[:, :, 64:65],
                        dparams[:, sk:sk + 4].rearrange("p (c o) -> p c o", o=1), ALU.add),
                        reads=[R_bank[ob[hh_]], R_dpar], writes=[rd.r], n=8)
                sc.op("dve", lambda e, rd=rd: e.reciprocal(rd.t[:], rd.t[:]), reads=[rd.r], writes=[rd.r], n=8)
                for hh_ in range(2):
                    sc.op("dve", lambda e, ov=ovs[hh_], hh_=hh_, otok=otok, rd=rd: e.tensor_tensor(
                        otok.t[:, hh_ * 256:(hh_ + 1) * 256].rearrange("p (c d) -> p c d", c=4), ov[:, :, 0:64],
                        rd.t[:, hh_ * 4:(hh_ + 1) * 4].rearrange("p (c o) -> p c o", o=1).to_broadcast([128, 4, 64]),
                        ALU.mult),
                        reads=[R_bank[ob[hh_]], rd.r], writes=[otok.r], n=256)
                tb = nbank()

                def emit_t(e, otok=otok, tb=tb):
                    ins = None
                    tv = banks[tb][:].bitcast(BF16)
                    for i in range(4):
                        ins = e.transpose(tv[:, i * 128:(i + 1) * 128], otok.t[:, i * 128:(i + 1) * 128],
                                          identb[:])
                    return ins
                sc.op("pe", emit_t, reads=[otok.r, R_const], writes=[R_bank[tb]], cost=0.35)
                sc.op("act", lambda e, tb=tb, tt=tt: e.activation(
                    hy[:, 4:8, tt * 128:(tt + 1) * 128],
                    banks[tb][:].bitcast(BF16)[:, 0:512].rearrange("p (i q) -> p i q", i=4), AF.Identity),
                    reads=[R_bank[tb]], writes=[RY(s, 4), RY(s, 5), RY(s, 6), RY(s, 7)])

        def wout_block(l, s):
            for m in range(KC):
                b = nbank()

                def emit(e, m=m, b=b):
                    lst = []
                    for kc in range(8):
                        p, loc = divmod(kc, 3)
                        lst.append((banks[b][:], ring[:, WOUT_SLOTS[p], loc * 1024 + m * 128:loc * 1024 + (m + 1) * 128],
                                    hy[:, kc, :]))
                    return _mm_group(e, lst)
                sc.op("pe", emit, reads=[R_slot[x] for x in WOUT_SLOTS] + [RY(s, c_) for c_ in range(8)],
                      writes=[R_bank[b]], cost=1.78)
                sc.op("dve", lambda e, m=m, b=b: e.tensor_tensor(
                    xr[:, m, blk(s)], banks[b][:], xr[:, m, blk(s)], ALU.add),
                    reads=[R_bank[b], R_xr[m][s]], writes=[R_xr[m][s]])

        def dump_xr():
            for k in range(KC):
                for s in range(NB):
                    sc.dma("sp", lambda e, k=k, s=s: e.dma_start(out=yout[k * 128:(k + 1) * 128, blk(s)],
                                                                 in_=xr[:, k, blk(s)]),
                           "out", reads=[R_xr[k][s]], writes=[R_out])

        for l in range(n_layers):
            last_layer = (l == n_layers - 1)

            def run_ffn(l, f, ln_i, final, next_loader):
                NG = len(GROUPS)
                pend = None
                for g in range(NG):
                    for s in range(NB):
                        hb = hcount[0] % 2
                        hcount[0] += 1
                        ffn_up(g, s, hb)
                        if pend is not None:
                            ffn_down(*pend)
                            if pend[0] == NG - 1:
                                ln_block(l, ln_i, pend[1], final)
                            if pend[0] == g - 1:
                                next_loader(g - 1)
                        pend = (g, s, hb)
                ffn_down(*pend)
                ln_block(l, ln_i, pend[1], final)
                next_loader(NG - 1)

            def loader_f1(g, l=l):
                if g + 2 < len(GROUPS):
                    load_ffn_group(l, 0, g + 2)
                elif g == len(GROUPS) - 2:
                    ensure_win(l, 0, 0)
                    load_diag(l)
                    if len(GROUPS[-1]) < 4:
                        ensure_win(l, 0, 1)
                else:
                    load_wout(l)

            sc.mark(f"L{l}.ffn1")
            run_ffn(l, 0, 0, False, loader_f1)
            sc.mark(f"L{l}.mixer")
            if stop_after == "ffn1":
                dump_xr()
                break
            sc.op("dve", lambda e: e.memset(ybuf[:, :, 0:30], 0.0), writes=R_ybuf)
            sc.op("dve", lambda e: e.memset(xbuf[:, :, 0:3], 0.0), writes=R_xbuf)
            for s in range(NB):
                mixer_block(l, s)
                if s == NB - 1:
                    load_ffn_group(l, 1, 0)
                wout_block(l, s)
                if ydbg is not None:
                    for c in range(8):
                        sc.dma("pool", lambda e, c=c, s=s: e.dma_start(out=ydbg[c * 128:(c + 1) * 128, blk(s)],
                                                                       in_=hy[:, c, :]),
                               "out", reads=[R_hy[c]], writes=[R_out])
                ln_block(l, 1, s, False)
            if stop_after == "mixer":
                dump_xr()
                break
            load_ffn_group(l, 1, 1)

            def loader_f2(g, l=l):
                if g + 2 < len(GROUPS):
                    load_ffn_group(l, 1, g + 2)
                elif not (l == n_layers - 1):
                    load_ffn_group(l + 1, 0, g + 2 - len(GROUPS))

            sc.mark(f"L{l}.ffn2")
            run_ffn(l, 1, 2, last_layer, loader_f2)

        import time as _time
        _t = _time.time()
        sc.schedule()
        print("sched stats:", {e: len(sc.order[e]) for e in ENGS}, "nf32", nf32,
              "makespan_us %.0f" % sc.makespan, "sched_s %.1f" % (_time.time() - _t))
        if SCHED_VERBOSE:
            mk = sc.marks + [("end", len(sc.ops))]
            for (nm, a), (_, b) in zip(mk[:-1], mk[1:]):
                seg = sc.ops[a:b]
                if not seg:
                    continue
                busy = {e: sum(o.cost for o in seg if o.eng == e and o.dsem is None) for e in ("pe", "act", "dve")}
                print("  phase %-10s start %7.0f end %7.0f  busy" % (nm, min(o.fin - o.cost for o in seg), max(o.fin for o in seg)),
                      {e: round(v) for e, v in busy.items()})
        if SIM_ONLY:
            return nc
        engs = {}
        with nc.Block() as block:
            @block.tensor
            def _(e):
                engs["pe"] = e
                sc.emit_all_one("pe", e)

            @block.vector
            def _(e):
                sc.emit_all_one("dve", e)

            @block.scalar
            def _(e):
                sc.emit_all_one("act", e)

            @block.gpsimd
            def _(e):
                sc.emit_all_one("pool", e)

            @block.sync
            def _(e):
                sc.emit_all_one("sp", e)
    return nc


def _rel_bucket_np(dist):
    max_exact = 16
    d = np.maximum(dist, 1).astype(np.float32)
    large = max_exact + (np.log(d / np.float32(max_exact)) / np.float32(np.log(128 / max_exact))
                         * np.float32(32 - max_exact)).astype(np.int32)
    large = np.minimum(large, 31)
    return np.where(dist < max_exact, dist, large)


def _prep_shared(inp):
    f32 = np.float32
    wst = np.zeros((DEPTH * UNITS_PER_LAYER, 128, SLOT_E), f32)
    qperm = []
    for c in range(4):
        qperm += list(range(1024 + c * 64, 1024 + (c + 1) * 64))
        qperm += list(range(1024 + (4 + c) * 64, 1024 + (5 + c) * 64))
    cols = list(range(1024)) + qperm + list(range(1536, 1792))
    for l in range(DEPTH):
        base = l * UNITS_PER_LAYER
        for f in range(2):
            wg = np.asarray(inp["ffn_w_gate"][l, f], f32).reshape(KC, 128, NCH, 128)
            wu = np.asarray(inp["ffn_w_up"][l, f], f32).reshape(KC, 128, NCH, 128)
            wd = np.asarray(inp["ffn_w_down"][l, f], f32).reshape(NCH, 128, D)
            u = wst[base + f * NCH: base + (f + 1) * NCH]
            u[:, :, 0:1024] = wg.transpose(2, 1, 0, 3).reshape(NCH, 128, 1024)
            u[:, :, 1024:2048] = wu.transpose(2, 1, 0, 3).reshape(NCH, 128, 1024)
            u[:, :, 2048:3072] = wd
        win = np.asarray(inp["w_in"][l], f32)[:, cols].reshape(KC, 128, 14, 128)
        winr = win.transpose(2, 1, 0, 3).reshape(14, 128, 1024)
        for cc in range(14):
            p, loc = divmod(cc, 3)
            wst[base + 2 * NCH + p, :, loc * 1024:(loc + 1) * 1024] = winr[cc]
        wo = np.asarray(inp["w_out"][l], f32).reshape(KC, 128, D)
        for kc in range(KC):
            p, loc = divmod(kc, 3)
            wst[base + 2 * NCH + 5 + p, :, loc * 1024:(loc + 1) * 1024] = wo[kc]
    for l in range(DEPTH):
        base = l * UNITS_PER_LAYER + 2 * NCH + 8
        cw = np.asarray(inp["conv_dw_w"][l], f32)
        ar = np.arange(128)
        for c in range(2):
            for j in range(31):
                i = c * 31 + j
                wst[base + i // 24, ar, (i % 24) * 128 + ar] = cw[j, c * 128:(c + 1) * 128]
        lw = np.asarray(inp["lru_conv_w"][l], f32)
        for c in range(2):
            for j in range(4):
                i = 62 + c * 4 + j
                wst[base + i // 24, ar, (i % 24) * 128 + ar] = lw[j, c * 128:(c + 1) * 128]
    P = np.zeros((128, DEPTH * NPL), f32)
    BD = np.zeros((128, DEPTH * 512), f32)
    for l in range(DEPTH):
        o = l * NPL
        P[:, o + P_LNG:o + P_LNG + 24] = np.asarray(inp["ln_g"][l], f32).reshape(3, 8, 128).transpose(2, 0, 1).reshape(128, 24)
        P[:, o + P_LNB:o + P_LNB + 24] = np.asarray(inp["ln_b"][l], f32).reshape(3, 8, 128).transpose(2, 0, 1).reshape(128, 24)
        P[:, o + P_CW:o + P_CW + 62] = np.asarray(inp["conv_dw_w"][l], f32).reshape(31, 2, 128).transpose(2, 1, 0).reshape(128, 62)
        P[:, o + P_CB:o + P_CB + 2] = np.asarray(inp["conv_dw_b"][l], f32).reshape(2, 128).T
        P[:, o + P_CG:o + P_CG + 2] = np.asarray(inp["conv_ln_g"][l], f32).reshape(2, 128).T
        P[:, o + P_CBT:o + P_CBT + 2] = np.asarray(inp["conv_ln_b"][l], f32).reshape(2, 128).T
        P[:, o + P_LW:o + P_LW + 8] = np.asarray(inp["lru_conv_w"][l], f32).reshape(4, 2, 128).transpose(2, 1, 0).reshape(128, 8)
        P[:, o + P_LB:o + P_LB + 2] = np.asarray(inp["lru_conv_b"][l], f32).reshape(2, 128).T
        P[:, o + P_BA:o + P_BA + 2] = np.asarray(inp["lru_ba"][l], f32).reshape(2, 128).T
        P[:, o + P_BX:o + P_BX + 2] = np.asarray(inp["lru_bx"][l], f32).reshape(2, 128).T
        P[:, o + P_LAM:o + P_LAM + 2] = np.asarray(inp["lru_lambda"][l], f32).reshape(2, 128).T
        P[:, o + P_SINK:o + P_SINK + 8] = np.broadcast_to(np.asarray(inp["attn_sinks"][l], f32)[None, :], (128, 8))
        for ax, nm in enumerate(("lru_wa", "lru_wx")):
            w = np.asarray(inp[nm][l], f32)
            for c in range(2):
                for hh in range(2):
                    BD[hh * 64:(hh + 1) * 64, l * 512 + ax * 256 + c * 128 + hh * 64: l * 512 + ax * 256 + c * 128 + (hh + 1) * 64] = w[2 * c + hh]
    rb = np.asarray(inp["rel_bias"], f32)
    kj = np.arange(128)[:, None]
    qi = np.arange(128)[None, :]
    biasT = np.zeros((128, 2, 4, 2, 128), f32)
    mask = np.zeros((128, 2, 128), f32)
    for part in range(2):
        dist = qi - kj + (128 if part == 0 else 0)
        valid = (dist >= 0) & (dist < 128)
        bucket = _rel_bucket_np(np.maximum(dist, 0))
        mask[:, part, :] = valid.astype(f32)
        for c in range(4):
            for hh in range(2):
                h = c + 4 * hh
                biasT[:, hh, c, part, :] = rb[bucket, h]
    return {
        "wst": wst, "prm": P, "bdd": BD,
        "biasT": np.ascontiguousarray(biasT.reshape(128, 2048)),
        "mask": np.ascontiguousarray(mask.reshape(128, 256)),
        "ident": np.eye(128, dtype=f32),
    }


_NC_CACHE = {}


def _get_nc(n_layers):
    if n_layers not in _NC_CACHE:
        _NC_CACHE[n_layers] = build_program(n_layers)
    return _NC_CACHE[n_layers]


FUSED = True


def kernel(**inputs):
    sh = _prep_shared(inputs)
    x = np.asarray(inputs["x"], np.float32)
    xT = [np.ascontiguousarray(x[b].T) for b in range(8)]
    if FUSED:
        nc = _get_nc(DEPTH)
        in_maps = [dict(sh, xin=xT[b]) for b in range(8)]
        res = run_bass_kernel_spmd(nc, in_maps, core_ids=list(range(8)))
        outs = [res.results[b]["yout"] for b in range(8)]
    else:
        nc = _get_nc(1)
        cur = xT
        for l in range(DEPTH):
            shl = dict(sh)
            shl["wst"] = np.ascontiguousarray(sh["wst"][l * UNITS_PER_LAYER:(l + 1) * UNITS_PER_LAYER])
            shl["prm"] = np.ascontiguousarray(sh["prm"][:, l * NPL:(l + 1) * NPL])
            shl["bdd"] = np.ascontiguousarray(sh["bdd"][:, l * 512:(l + 1) * 512])
            in_maps = [dict(shl, xin=cur[b]) for b in range(8)]
            res = run_bass_kernel_spmd(nc, in_maps, core_ids=list(range(8)))
            cur = [np.ascontiguousarray(res.results[b]["yout"]) for b in range(8)]
        outs = cur
    return np.stack([np.ascontiguousarray(o.T) for o in outs], axis=0).astype(np.float32)
```

```python
import contextlib
import numpy as np
import concourse.bass as bass
import concourse.mybir as mybir
from concourse.bass_utils import run_bass_kernel_spmd

F32 = mybir.dt.float32
BF16 = mybir.dt.bfloat16
F32R = mybir.dt.float32r
AF = mybir.ActivationFunctionType
ALU = mybir.AluOpType

D = 1024
S = 2048
DEPTH = 2
DFF = 2816
NCH = DFF // 128
KC = D // 128
TB = 512
NB = S // TB
ALPHA = (2.0 * DEPTH) ** 0.25
EPS = 1e-5
GROUPS = [[0, 1, 2, 3], [4, 5, 6, 7], [8, 9, 10], [11, 12, 13], [14, 15, 16, 17], [18, 19, 20, 21]]
SLOT_E = 3072
NSLOT = 8
UNITS_PER_LAYER = 2 * NCH + 5 + 3 + 3
DIAG_SLOTS = [0, 1, 2]
WIN_STREAM = [3, 7]
WOUT_SLOTS = [4, 5, 6]
NPL = 140
P_LNG, P_LNB, P_CW, P_CB, P_CG, P_CBT, P_LW, P_LB, P_BA, P_BX, P_LAM, P_SINK = (
    0, 24, 48, 110, 112, 114, 116, 124, 126, 128, 130, 132)
GELU_C = 0.7978845608028654 * 2.0
SCHED_VERBOSE = False
SIM_ONLY = False
SIM_NF32 = 0
SIM_NBF = 0
SIM_NBANK = 8
SIM_YMIX2 = 0


class Res:
    __slots__ = ("name", "w", "r", "const")

    def __init__(self, name):
        self.name = name
        self.w = None
        self.r = []
        self.const = False


class Op:
    __slots__ = ("i", "eng", "emit", "deps", "cost", "dsem", "pos", "fin", "crit", "start", "tab")

    def __init__(self, i, eng, emit, deps, cost, dsem):
        self.i, self.eng, self.emit, self.deps, self.cost, self.dsem = i, eng, emit, deps, cost, dsem
        self.pos = None
        self.fin = None


ENGS = ("pe", "act", "dve", "pool", "sp")
ACT_TABS = {"Silu": "silu", "Sigmoid": "sig", "Exp": "exp", "Sqrt": "sqrt", "Ln": "exp"}
TAB_COST = 1.3
FIFO_ENGS = ("sp",)
LN_POOL = 1
POOL_TT_COST = 1.45
POOL_DMA_ISSUE = 1.2
SCHED_WINDOW = 24
ATTACH_WAIT = 1
CP_PRIO = 1
SCHED_LAT = 0.25
STRICT_SAME_ENGINE = 0
DMA_RATE = 170e3
DMA_LAT = 3.0


class Sched:
    def __init__(self, nc, stack):
        self.nc = nc
        self.ops = []
        self.sem = {e: stack.enter_context(nc.semaphore("s_" + e)) for e in ENGS}
        self.dsem = {}
        self.stack = stack
        self.marks = []

    def mark(self, name):
        self.marks.append((name, len(self.ops)))

    def new_dsem(self, name):
        self.dsem[name] = self.stack.enter_context(self.nc.semaphore("d_" + name))

    def _record(self, eng, emit, reads, writes, cost, dsem):
        i = len(self.ops)
        deps = {}
        for r in reads:
            if r.w is not None:
                deps[r.w] = True
        for w in writes:
            if w.w is not None:
                deps.setdefault(w.w, False)
            for t in w.r:
                deps.setdefault(t, False)
        deps.pop(i, None)
        o_ = Op(i, eng, emit, deps, cost, dsem)
        o_.tab = None
        if eng == "act":
            for nm in emit.__code__.co_names:
                if nm in ACT_TABS:
                    o_.tab = ACT_TABS[nm]
                    break
        self.ops.append(o_)
        for r in reads:
            if not r.const:
                r.r.append(i)
        for w in writes:
            w.w = i
            w.r = []
        return i

    def op(self, eng, emit, reads=(), writes=(), cost=None, n=512):
        if cost is None:
            if eng == "dve":
                cost = 0.12 + n / 960.0
            elif eng == "act":
                cost = 0.22 + n / 1200.0
            else:
                cost = 0.5
        return self._record(eng, emit, reads, writes, cost, None)

    def dma(self, qeng, emit, dname, reads=(), writes=(), nbytes=262144):
        return self._record(qeng, emit, reads, writes, nbytes / DMA_RATE, dname)

    def schedule(self):
        ops = self.ops
        n = len(ops)
        pend = {e: [] for e in ENGS}
        for o in ops:
            pend[o.eng].append(o.i)
        head = {e: 0 for e in ENGS}
        done = [False] * n
        tfree = {e: 0.0 for e in ENGS}
        dma_free = 0.0
        order = {e: [] for e in ENGS}
        remaining = n
        cur_tab = None
        tail = [0.0] * n
        if CP_PRIO:
            dependents = [[] for _ in range(n)]
            for o in ops:
                for d in o.deps:
                    dependents[d].append(o.i)
            for i in range(n - 1, -1, -1):
                m = 0.0
                for j in dependents[i]:
                    if tail[j] > m:
                        m = tail[j]
                tail[i] = ops[i].cost + (DMA_LAT if ops[i].dsem is not None else 0.0) + m
        while remaining:
            best = None
            for e in ENGS:
                lst = pend[e]
                h = head[e]
                while h < len(lst) and done[lst[h]]:
                    h += 1
                head[e] = h
                if h >= len(lst):
                    continue
                lim = 1 if e in FIFO_ENGS else SCHED_WINDOW
                cnt = 0
                j = h
                te = tfree[e]
                while j < len(lst) and cnt < lim:
                    idx = lst[j]
                    j += 1
                    if done[idx]:
                        continue
                    cnt += 1
                    o = ops[idx]
                    ready = te
                    ok = True
                    crit = -1
                    for d, raw in o.deps.items():
                        od = ops[d]
                        f = od.fin
                        if f is None:
                            ok = False
                            break
                        if od.eng != e or od.dsem is not None or ((raw or STRICT_SAME_ENGINE) and e != "pe"):
                            f += SCHED_LAT
                        if f > ready:
                            ready = f
                            crit = d
                    if not ok:
                        continue
                    if o.tab is not None and o.tab != cur_tab:
                        ready += TAB_COST
                    o.crit = crit
                    if CP_PRIO:
                        key = (max(ready, te), -tail[idx], idx)
                        if best is None or key < best[2]:
                            best = ((ready, idx), e, key)
                    else:
                        if best is None or (ready, idx) < best[0]:
                            best = ((ready, idx), e, None)
                        if ready <= te:
                            break
            assert best is not None, "scheduler stuck"
            (ready, idx), e = best[0], best[1]
            o = ops[idx]
            if o.dsem is not None:
                tfree[e] = ready + (POOL_DMA_ISSUE if e == "pool" else 0.06)
                st_ = max(ready, dma_free)
                o.fin = st_ + o.cost + DMA_LAT
                dma_free = st_ + o.cost
            else:
                if o.tab is not None:
                    cur_tab = o.tab
                o.fin = ready + o.cost
                tfree[e] = o.fin
            o.start = ready
            if o.crit == -1 and order[e]:
                o.crit = -2 - order[e][-1]
            done[idx] = True
            order[e].append(idx)
            remaining -= 1
        self.order = order
        self.makespan = max(o.fin for o in ops)

    def _assign(self):
        ops = self.ops
        pos = [0] * len(ops)
        for e in ENGS:
            for p, idx in enumerate(self.order[e]):
                pos[idx] = p + 1
        self.waits = {}
        needed = set()
        for e in ENGS:
            waited = {}
            for idx in self.order[e]:
                o = ops[idx]
                strict = o.dsem is not None
                wl = []
                for d, raw in o.deps.items():
                    od = ops[d]
                    if od.dsem is None and od.eng == e and not strict:
                        if e == "pe" or (not raw and not STRICT_SAME_ENGINE):
                            continue
                    key = ("d_" + od.dsem) if od.dsem is not None else od.eng
                    if waited.get(key, 0) >= pos[d]:
                        continue
                    waited[key] = pos[d]
                    wl.append(d)
                    needed.add(d)
                self.waits[idx] = wl
        cnt = {e: 0 for e in ENGS}
        dcnt = {k: 0 for k in self.dsem}
        tok = [None] * len(ops)
        for e in ENGS:
            for idx in self.order[e]:
                o = ops[idx]
                if o.dsem is not None:
                    dcnt[o.dsem] += 16
                    tok[idx] = (self.dsem[o.dsem], dcnt[o.dsem])
                elif idx in needed:
                    cnt[e] += 1
                    tok[idx] = (self.sem[e], cnt[e])
        self.tok = tok
        self.dcnt_final = dcnt
        self.ninc = len(needed)

    def emit_all_one(self, e, eng):
        if not hasattr(self, "tok"):
            self._assign()
        ops = self.ops
        tok = self.tok
        for idx in self.order[e]:
            o = ops[idx]
            wl = self.waits[idx]
            attach = None
            if ATTACH_WAIT and wl and e in ("act", "dve") and o.dsem is None:
                attach = wl[-1]
                wl = wl[:-1]
            for d in wl:
                sem, val = tok[d]
                eng.wait_ge(sem, val)
            ins = o.emit(eng)
            if attach is not None:
                sem, val = tok[attach]
                ins._wait_ge(sem, val)
            if tok[idx] is not None:
                ins.then_inc(tok[idx][0], 16 if o.dsem is not None else 1)
        if e == "sp":
            for k, v in self.dcnt_final.items():
                if k == "out" and v > 0:
                    eng.wait_ge(self.dsem[k], v)


class Pool:
    def __init__(self, nc, stack, name, n, shape, dt):
        self.tiles = [stack.enter_context(nc.sbuf_tensor(f"{name}{i}", shape, dt)) for i in range(n)]
        self.res = [Res(f"{name}{i}") for i in range(n)]
        self.gen = [0] * n
        self.i = 0
        self.n = n

    def get(self):
        i = self.i
        self.i = (self.i + 1) % self.n
        self.gen[i] += 1
        return TRef(self, i, self.gen[i])


class TRef:
    __slots__ = ("pool", "i", "g")

    def __init__(self, pool, i, g):
        self.pool, self.i, self.g = pool, i, g

    @property
    def t(self):
        return self.pool.tiles[self.i]

    @property
    def r(self):
        assert self.pool.gen[self.i] == self.g, "stale scratch tile"
        return self.pool.res[self.i]


def build_program(n_layers, stop_after=None, skip=()):
    nc = bass.Bass("TRN2", target_bir_lowering=False)
    NU = n_layers * UNITS_PER_LAYER
    xin = nc.dram_tensor("xin", [D, S], F32, kind="ExternalInput").ap()
    wst = nc.dram_tensor("wst", [NU, 128, SLOT_E], F32, kind="ExternalInput").ap()
    prm = nc.dram_tensor("prm", [128, n_layers * NPL], F32, kind="ExternalInput").ap()
    bdd = nc.dram_tensor("bdd", [128, n_layers * 512], F32, kind="ExternalInput").ap()
    bias_d = nc.dram_tensor("biasT", [128, 2048], F32, kind="ExternalInput").ap()
    mask_d = nc.dram_tensor("mask", [128, 256], F32, kind="ExternalInput").ap()
    ident_d = nc.dram_tensor("ident", [128, 128], F32, kind="ExternalInput").ap()
    yout = nc.dram_tensor("yout", [D, S], F32, kind="ExternalOutput").ap()
    ydbg = nc.dram_tensor("ydbg", [D, S], F32, kind="ExternalOutput").ap() if stop_after == "mixer" else None

    with contextlib.ExitStack() as st:
        sc = Sched(nc, st)
        for i in range(NSLOT):
            sc.new_dsem(f"slot{i}")
        sc.new_dsem("const")
        for k in range(KC):
            sc.new_dsem(f"xin{k}")
        sc.new_dsem("out")
        for l in range(n_layers):
            sc.new_dsem(f"bd{l}")

        def sb(name, shape, dt):
            return st.enter_context(nc.sbuf_tensor(name, shape, dt))

        xr = sb("xr", [128, KC, S if not SIM_NF32 else S // 4], F32)
        xb = sb("xb", [128, KC, S], BF16)
        ring = sb("ring", [128, NSLOT, SLOT_E], BF16)
        hy = sb("hy", [128, 8, TB], BF16)
        params = sb("params", [128, n_layers * NPL], F32)
        dparams = sb("dparams", [128, n_layers * 64], F32)
        bdb = sb("bdb", [128, n_layers * 512], BF16)
        EB = sb("EB", [128, 2048], BF16)
        maskt = sb("maskt", [128, 256], F32)
        ident = sb("ident_s", [128, 128], F32)
        ones = sb("ones_s", [128, 128], F32R)
        ones_f = sb("ones_f", [128, 128], F32)
        identb = sb("identb", [128, 128], BF16)
        qsb = sb("qsb", [128, 4, TB], BF16)
        ksb = sb("ksb", [128, S], BF16)
        vsb = sb("vsb", [128, 16, 2, 65], BF16)
        ybuf = sb("ybuf", [128, 2, 30 + TB], BF16)
        xbuf = sb("xbuf", [128, 2, 3 + TB], BF16)
        hlast = sb("hlast", [128, 2], F32)
        rden = sb("rden", [128, 8], F32)
        banks = [st.enter_context(nc.psum_tensor(f"bank{i}", [128, 512], F32)) for i in range(8)]
        if SIM_NBANK > 8:
            banks = [banks[i % 8] for i in range(SIM_NBANK)]

        nbf = SIM_NBF or 4
        pb16 = Pool(nc, st, "tb", nbf, [128, TB], BF16)
        rdp = Pool(nc, st, "rd", 4, [128, 8], F32)
        pfr = Pool(nc, st, "tr", 3, [128, TB], F32R)
        nfree = nc.sbuf_bytes_remaining
        nf32 = SIM_NF32 or min(14, (nfree - 1024) // 2048)
        assert nf32 >= 7, f"not enough SBUF for scratch tiles: {nf32}"
        pf32 = Pool(nc, st, "tf", nf32, [128, TB], F32)

        R_xr = [[Res(f"xr{k}_{s}") for s in range(NB)] for k in range(KC)]
        R_xb = [[Res(f"xb{k}_{s}") for s in range(NB)] for k in range(KC)]
        R_slot = [Res(f"slot{i}") for i in range(NSLOT)]
        R_hy = [Res(f"hy{i}") for i in range(8)]
        R_bank = [Res(f"bank{i}") for i in range(SIM_NBANK)]
        R_ym2 = [[Res(f"ym{p}_{i}") for i in range(8)] for p in range(2)]

        def RY(s, c):
            return R_ym2[s % 2][c] if SIM_YMIX2 else R_hy[c]
        R_const = Res("const")
        R_dpar = Res("dpar")
        R_qsb = Res("qsb")
        R_ksb = [Res(f"ksb{s}") for s in range(NB)]
        R_vsb = [Res(f"vsb{s}") for s in range(NB)]
        R_ybuf = [Res("ybuf0"), Res("ybuf1")]
        R_xbuf = [Res("xbuf0"), Res("xbuf1")]
        R_hlast = [Res("hlast0"), Res("hlast1")]
        R_rden = Res("rden")
        R_out = Res("out")
        bank_i = [0]

        def nbank():
            i = bank_i[0]
            bank_i[0] = (i + 1) % SIM_NBANK
            return i

        def blk(s):
            return slice(s * TB, (s + 1) * TB)

        xin_v = xin.rearrange("(k p) t -> p k t", p=128)
        for s in range(NB):
            sc.dma("sp", lambda e, s=s: e.dma_start(out=xr[:, :, blk(s)], in_=xin_v[:, :, blk(s)]),
                   f"xin{s}", writes=[R_xr[k][s] for k in range(KC)], nbytes=128 * KC * TB * 4)

        stg0 = pf32.get()
        stg1 = pf32.get()
        sc.dma("sp", lambda e: e.dma_start(out=params[:], in_=prm[:, :]), "const", writes=[R_const])
        ebst = [pf32.get() for _ in range(4)]
        for qi_ in range(4):
            sc.new_dsem(f"eb{qi_}")
            sc.dma("sp", lambda e, qi_=qi_: e.dma_start(out=ebst[qi_].t[:], in_=bias_d[:, qi_ * 512:(qi_ + 1) * 512]),
                   f"eb{qi_}", writes=[ebst[qi_].r])
        sc.dma("sp", lambda e: e.dma_start(out=maskt[:], in_=mask_d[:, :]), "const", writes=[R_const])
        for l in range(n_layers):
            tl = stg0 if l == 0 else stg1
            sc.dma("sp", lambda e, tl=tl, l=l: e.dma_start(out=tl.t[:], in_=bdd[:, l * 512:(l + 1) * 512]),
                   f"bd{l}", writes=[tl.r])
        def load_unit(unit, slot):
            def emit(e, unit=unit, slot=slot):
                return e.dma_start(
                    out=ring[:, slot, :].rearrange("p (a b) -> p a b", a=2),
                    in_=wst[unit].rearrange("p (a b) -> p a b", a=2))
            sc.dma("pool", emit, f"slot{slot}", writes=[R_slot[slot]], nbytes=128 * SLOT_E * 4)

        def load_ffn_group(l, f, g):
            for i, j in enumerate(GROUPS[g]):
                load_unit(l * UNITS_PER_LAYER + f * NCH + j, (g % 2) * 4 + i)

        win_loaded = {}
        win_n = [0]

        def ensure_win(l, s, p):
            if s >= NB or p >= 5:
                return
            if (l, s, p) in win_loaded:
                return
            slot = WIN_STREAM[win_n[0] % 2]
            win_n[0] += 1
            win_loaded[(l, s, p)] = slot
            load_unit(l * UNITS_PER_LAYER + 2 * NCH + p, slot)

        def load_diag(l):
            for p in range(3):
                load_unit(l * UNITS_PER_LAYER + 2 * NCH + 8 + p, DIAG_SLOTS[p])

        def load_wout(l):
            for p in range(3):
                load_unit(l * UNITS_PER_LAYER + 2 * NCH + 5 + p, WOUT_SLOTS[p])

        load_ffn_group(0, 0, 0)
        load_ffn_group(0, 0, 1)

        sc.op("dve", lambda e: e.memset(ones_f[:], 1.0), writes=[R_const])
        sc.op("dve", lambda e: e.tensor_copy(ones[:], ones_f[:]), reads=[R_const], writes=[R_const])
        sc.new_dsem("ident")
        sc.dma("sp", lambda e: e.dma_start(out=ident[:], in_=ident_d[:, :]), "ident", reads=[R_const], writes=[R_const])
        sc.op("act", lambda e: e.activation(identb[:], ident[:], AF.Identity), reads=[R_const], writes=[R_const])
        sc.op("dve", lambda e: e.memset(vsb[:, :, :, 64:65], 1.0), writes=R_vsb)
        for qi_ in range(4):
            sc.op("act", lambda e, qi_=qi_: e.activation(ebst[qi_].t[:], ebst[qi_].t[:], AF.Exp),
                  reads=[ebst[qi_].r], writes=[ebst[qi_].r])
            for h2 in range(2):
                sc.op("dve", lambda e, qi_=qi_, h2=h2: e.tensor_tensor(
                    EB[:, qi_ * 512 + h2 * 256:qi_ * 512 + (h2 + 1) * 256],
                    ebst[qi_].t[:, h2 * 256:(h2 + 1) * 256], maskt[:], ALU.mult),
                    reads=[ebst[qi_].r, R_const], writes=[R_const])
        for l in range(n_layers):
            tl = stg0 if l == 0 else stg1
            sc.op("act", lambda e, tl=tl, l=l: e.activation(bdb[:, l * 512:(l + 1) * 512], tl.t[:], AF.Identity),
                  reads=[tl.r], writes=[R_const])
        for l in range(n_layers):
            po = l * NPL
            do = l * 64
            sc.op("dve", lambda e, po=po, do=do: e.tensor_scalar(
                dparams[:, do:do + 48], params[:, po:po + 48], float(ALPHA), None, ALU.mult),
                reads=[R_const], writes=[R_dpar])
            sc.op("act", lambda e, po=po, do=do: e.activation(
                dparams[:, do + 48:do + 50], params[:, po + P_LAM:po + P_LAM + 2], AF.Exp, scale=-1.0),
                reads=[R_const], writes=[R_dpar])
            sc.op("act", lambda e, do=do: e.activation(
                dparams[:, do + 48:do + 50], dparams[:, do + 48:do + 50], AF.Ln, bias=1.0),
                reads=[R_dpar], writes=[R_dpar])
            sc.op("dve", lambda e, do=do: e.tensor_scalar(
                dparams[:, do + 50:do + 52], dparams[:, do + 48:do + 50], -16.0, None, ALU.mult),
                reads=[R_dpar], writes=[R_dpar])
            sc.op("dve", lambda e, do=do: e.tensor_scalar(
                dparams[:, do + 48:do + 50], dparams[:, do + 48:do + 50], -8.0, None, ALU.mult),
                reads=[R_dpar], writes=[R_dpar])
            sc.op("act", lambda e, po=po, do=do: e.activation(
                dparams[:, do + 52:do + 60], params[:, po + P_SINK:po + P_SINK + 8], AF.Exp),
                reads=[R_const], writes=[R_dpar])
        for k in range(KC):
            for s in range(NB):
                sc.op("act", lambda e, k=k, s=s: e.activation(xb[:, k, blk(s)], xr[:, k, blk(s)], AF.Identity),
                      reads=[R_xr[k][s]], writes=[R_xb[k][s]])
                sc.op("dve", lambda e, k=k, s=s: e.tensor_scalar(
                    xr[:, k, blk(s)], xr[:, k, blk(s)], float(ALPHA), None, ALU.mult),
                    reads=[R_xr[k][s]], writes=[R_xr[k][s]])

        R_const.const = True
        R_dpar.const = True

        def ln_block(l, i, s, final):
            po = l * NPL
            do = l * 64
            b1 = nbank()
            b2 = nbank()
            for k in range(KC):
                xc = pfr.get()
                sc.op("act", lambda e, xc=xc, k=k: e.activation(xc.t[:], xr[:, k, blk(s)], AF.Identity),
                      reads=[R_xr[k][s]], writes=[xc.r])
                sc.op("pe", lambda e, xc=xc, k=k: e.matmul(banks[b1][:], ones[:], xc.t[:],
                                                            start=(k == 0), stop=(k == KC - 1)),
                      reads=[xc.r, R_const], writes=[R_bank[b1]], cost=0.23)
                sq = pfr.get()
                sc.op("act", lambda e, sq=sq, k=k: e.activation(sq.t[:], xr[:, k, blk(s)], AF.Square),
                      reads=[R_xr[k][s]], writes=[sq.r])
                sc.op("pe", lambda e, sq=sq, k=k: e.matmul(banks[b2][:], ones[:], sq.t[:],
                                                            start=(k == 0), stop=(k == KC - 1)),
                      reads=[sq.r, R_const], writes=[R_bank[b2]], cost=0.23)
            mean = pf32.get()
            sc.op("dve", lambda e: e.tensor_scalar(mean.t[:], banks[b1][:], 1.0 / D, None, ALU.mult),
                  reads=[R_bank[b1]], writes=[mean.r])
            msq = pf32.get()
            sc.op("dve", lambda e: e.tensor_tensor(msq.t[:], mean.t[:], mean.t[:], ALU.mult),
                  reads=[mean.r], writes=[msq.r])
            sc.op("dve", lambda e: e.scalar_tensor_tensor(msq.t[:], banks[b2][:], 1.0 / D, msq.t[:],
                                                          ALU.mult, ALU.subtract),
                  reads=[R_bank[b2], msq.r], writes=[msq.r])
            sc.op("act", lambda e: e.activation(msq.t[:], msq.t[:], AF.Ln, bias=EPS),
                  reads=[msq.r], writes=[msq.r])
            sc.op("act", lambda e: e.activation(msq.t[:], msq.t[:], AF.Exp, scale=-0.5),
                  reads=[msq.r], writes=[msq.r])
            rstd = msq
            sc.op("dve", lambda e: e.tensor_tensor(mean.t[:], mean.t[:], rstd.t[:], ALU.mult),
                  reads=[mean.r, rstd.r], writes=[mean.r])
            mrs = mean
            for k in range(KC):
                if LN_POOL:
                    sc.op("pool", lambda e, k=k: e.tensor_tensor(xr[:, k, blk(s)], xr[:, k, blk(s)], rstd.t[:], ALU.mult),
                          reads=[R_xr[k][s], rstd.r], writes=[R_xr[k][s]], cost=POOL_TT_COST)
                else:
                    sc.op("dve", lambda e, k=k: e.tensor_tensor(xr[:, k, blk(s)], xr[:, k, blk(s)], rstd.t[:], ALU.mult),
                          reads=[R_xr[k][s], rstd.r], writes=[R_xr[k][s]])
                sc.op("dve", lambda e, k=k: e.tensor_tensor(xr[:, k, blk(s)], xr[:, k, blk(s)], mrs.t[:], ALU.subtract),
                      reads=[R_xr[k][s], mrs.r], writes=[R_xr[k][s]])
                gc = po + P_LNG + i * 8 + k
                bc = po + P_LNB + i * 8 + k
                if final:
                    sc.op("act", lambda e, k=k, gc=gc, bc=bc: e.activation(
                        xr[:, k, blk(s)], xr[:, k, blk(s)], AF.Identity, bias=params[:, bc:bc + 1],
                        scale=params[:, gc:gc + 1]),
                        reads=[R_xr[k][s], R_const], writes=[R_xr[k][s]])
                    sc.dma("sp", lambda e, k=k: e.dma_start(out=yout[k * 128:(k + 1) * 128, blk(s)],
                                                            in_=xr[:, k, blk(s)]),
                           "out", reads=[R_xr[k][s]], writes=[R_out])
                else:
                    ag = do + i * 8 + k
                    ab = do + 24 + i * 8 + k
                    sc.op("act", lambda e, k=k, gc=gc, bc=bc: e.activation(
                        xb[:, k, blk(s)], xr[:, k, blk(s)], AF.Identity, bias=params[:, bc:bc + 1],
                        scale=params[:, gc:gc + 1]),
                        reads=[R_xr[k][s], R_const], writes=[R_xb[k][s]])
                    sc.op("dve", lambda e, k=k, ag=ag, ab=ab: e.tensor_scalar(
                        xr[:, k, blk(s)], xr[:, k, blk(s)], dparams[:, ag:ag + 1], dparams[:, ab:ab + 1],
                        ALU.mult, ALU.add),
                        reads=[R_xr[k][s], R_dpar], writes=[R_xr[k][s]])

        def _mm_group(e, lst):
            ins = None
            n = len(lst)
            for i, (o, a, b) in enumerate(lst):
                ins = e.matmul(o, a, b, start=(i == 0), stop=(i == n - 1))
            return ins

        def ffn_up(g, s, hb):
            for jj, j in enumerate(GROUPS[g]):
                slot = (g % 2) * 4 + jj
                bg = nbank()
                sc.op("pe", lambda e, slot=slot, bg=bg: _mm_group(e, [
                    (banks[bg][:], ring[:, slot, k * 128:(k + 1) * 128], xb[:, k, blk(s)]) for k in range(KC)]),
                    reads=[R_slot[slot]] + [R_xb[k][s] for k in range(KC)], writes=[R_bank[bg]], cost=1.78)
                bu = nbank()
                sc.op("pe", lambda e, slot=slot, bu=bu: _mm_group(e, [
                    (banks[bu][:], ring[:, slot, 1024 + k * 128:1024 + (k + 1) * 128], xb[:, k, blk(s)])
                    for k in range(KC)]),
                    reads=[R_slot[slot]] + [R_xb[k][s] for k in range(KC)], writes=[R_bank[bu]], cost=1.78)
                sg = pb16.get()
                sc.op("act", lambda e, sg=sg, bg=bg: e.activation(sg.t[:], banks[bg][:], AF.Silu),
                      reads=[R_bank[bg]], writes=[sg.r])
                sc.op("dve", lambda e, sg=sg, bu=bu, jj=jj: e.scalar_tensor_tensor(
                    hy[:, hb * 4 + jj, :], sg.t[:], 0.5, banks[bu][:], ALU.mult, ALU.mult),
                    reads=[sg.r, R_bank[bu]], writes=[R_hy[hb * 4 + jj]])

        def ffn_down(g, s, hb):
            ng = len(GROUPS[g])
            for m in range(KC):
                by = nbank()
                sc.op("pe", lambda e, m=m, by=by: _mm_group(e, [
                    (banks[by][:], ring[:, (g % 2) * 4 + jj, 2048 + m * 128:2048 + (m + 1) * 128],
                     hy[:, hb * 4 + jj, :]) for jj in range(ng)]),
                    reads=[R_slot[(g % 2) * 4 + jj] for jj in range(ng)] + [R_hy[hb * 4 + jj] for jj in range(ng)],
                    writes=[R_bank[by]], cost=0.2225 * ng)
                sc.op("dve", lambda e, m=m, by=by: e.tensor_tensor(
                    xr[:, m, blk(s)], banks[by][:], xr[:, m, blk(s)], ALU.add),
                    reads=[R_bank[by], R_xr[m][s]], writes=[R_xr[m][s]])

        hcount = [0]

        cur_l = [0]

        def u_chunk_ap(cc, k, s):
            p, loc = divmod(cc, 3)
            return win_loaded[(cur_l[0], s, p)], loc * 1024 + k * 128

        def u_prefetch(cc, s):
            p = cc // 3
            ensure_win(cur_l[0], s, p)
            if p + 1 < 5:
                ensure_win(cur_l[0], s, p + 1)
            else:
                ensure_win(cur_l[0], s + 1, 0)

        def u_matmul(cc, s):
            u_prefetch(cc, s)
            b = nbank()
            slot0, _ = u_chunk_ap(cc, 0, s)

            def emit(e, cc=cc, b=b):
                lst = []
                for k in range(KC):
                    slot, off = u_chunk_ap(cc, k, s)
                    lst.append((banks[b][:], ring[:, slot, off:off + 128], xb[:, k, blk(s)]))
                return _mm_group(e, lst)
            sc.op("pe", emit, reads=[R_slot[slot0]] + [R_xb[k][s] for k in range(KC)], writes=[R_bank[b]],
                  cost=1.8)
            return b

        def mixer_block(l, s):
            cur_l[0] = l
            po = l * NPL
            do = l * 64
            bo = l * 512
            if 'conv' not in skip:
              _conv_branch(l, s, po, do, bo)
            if 'lru' not in skip:
              _lru_branch(l, s, po, do, bo)
            if 'attn' not in skip:
              _attn_branch(l, s, po, do, bo)

        def _conv_branch(l, s, po, do, bo):
            ba = [u_matmul(0, s), u_matmul(1, s)]
            bg = [u_matmul(2, s), u_matmul(3, s)]
            accs = []
            for c in range(2):
                sig = pf32.get()
                sc.op("act", lambda e, sig=sig, c=c: e.activation(sig.t[:], banks[bg[c]][:], AF.Sigmoid),
                      reads=[R_bank[bg[c]]], writes=[sig.r])
                sc.op("dve", lambda e, sig=sig, c=c: e.tensor_tensor(
                    ybuf[:, c, 30:30 + TB], banks[ba[c]][:], sig.t[:], ALU.mult),
                    reads=[R_bank[ba[c]], sig.r], writes=[R_ybuf[c]])
                acc = pf32.get()
                cb = po + P_CB + c
                bc = nbank()

                def emit_conv(e, c=c, bc=bc):
                    ins = None
                    for j in range(31):
                        i = c * 31 + j
                        ins = e.matmul(banks[bc][:], ring[:, DIAG_SLOTS[i // 24], (i % 24) * 128:(i % 24 + 1) * 128],
                                       ybuf[:, c, j:j + TB], start=(j == 0), stop=(j == 30))
                    return ins
                sc.op("pe", emit_conv, reads=[R_slot[x] for x in DIAG_SLOTS] + [R_ybuf[c]], writes=[R_bank[bc]],
                      cost=31 * 0.222)
                sc.op("act", lambda e, acc=acc, bc=bc, cb=cb: e.activation(
                    acc.t[:], banks[bc][:], AF.Identity, bias=params[:, cb:cb + 1]),
                    reads=[R_bank[bc], R_const], writes=[acc.r])
                sc.op("dve", lambda e, c=c: e.tensor_copy(ybuf[:, c, 0:30], ybuf[:, c, TB:TB + 30]),
                      reads=[R_ybuf[c]], writes=[R_ybuf[c]], n=30)
                accs.append(acc)
            b1 = nbank()
            b2 = nbank()
            for c in range(2):
                ar = pfr.get()
                sc.op("act", lambda e, ar=ar, c=c: e.activation(ar.t[:], accs[c].t[:], AF.Identity),
                      reads=[accs[c].r], writes=[ar.r])
                sc.op("pe", lambda e, ar=ar, c=c: e.matmul(banks[b1][:], ones[:], ar.t[:], start=(c == 0), stop=(c == 1)),
                      reads=[ar.r, R_const], writes=[R_bank[b1]], cost=0.23)
                sq = pfr.get()
                sc.op("act", lambda e, sq=sq, c=c: e.activation(sq.t[:], accs[c].t[:], AF.Square),
                      reads=[accs[c].r], writes=[sq.r])
                sc.op("pe", lambda e, sq=sq, c=c: e.matmul(banks[b2][:], ones[:], sq.t[:], start=(c == 0), stop=(c == 1)),
                      reads=[sq.r, R_const], writes=[R_bank[b2]], cost=0.23)
            mean2 = pf32.get()
            sc.op("dve", lambda e: e.tensor_scalar(mean2.t[:], banks[b1][:], 1.0 / 256, None, ALU.mult),
                  reads=[R_bank[b1]], writes=[mean2.r])
            var = pf32.get()
            sc.op("dve", lambda e: e.tensor_tensor(var.t[:], mean2.t[:], mean2.t[:], ALU.mult),
                  reads=[mean2.r], writes=[var.r])
            sc.op("dve", lambda e: e.scalar_tensor_tensor(var.t[:], banks[b2][:], 1.0 / 256, var.t[:],
                                                          ALU.mult, ALU.subtract),
                  reads=[R_bank[b2], var.r], writes=[var.r])
            sc.op("act", lambda e: e.activation(var.t[:], var.t[:], AF.Ln, bias=EPS),
                  reads=[var.r], writes=[var.r])
            sc.op("act", lambda e: e.activation(var.t[:], var.t[:], AF.Exp, scale=-0.5),
                  reads=[var.r], writes=[var.r])
            sc.op("dve", lambda e: e.tensor_tensor(mean2.t[:], mean2.t[:], var.t[:], ALU.mult),
                  reads=[mean2.r, var.r], writes=[mean2.r])
            for c in range(2):
                acc = accs[c]
                sc.op("dve", lambda e, acc=acc: e.tensor_tensor(acc.t[:], acc.t[:], var.t[:], ALU.mult),
                      reads=[acc.r, var.r], writes=[acc.r])
                sc.op("dve", lambda e, acc=acc: e.tensor_tensor(acc.t[:], acc.t[:], mean2.t[:], ALU.subtract),
                      reads=[acc.r, mean2.r], writes=[acc.r])
                cg = po + P_CG + c
                cbt = po + P_CBT + c
                sc.op("act", lambda e, acc=acc, c=c, cg=cg, cbt=cbt: e.activation(
                    hy[:, c, :], acc.t[:], AF.Silu, bias=params[:, cbt:cbt + 1], scale=params[:, cg:cg + 1]),
                    reads=[acc.r, R_const], writes=[RY(s, c)])


        def _lru_branch(l, s, po, do, bo):
            bx = [u_matmul(4, s), u_matmul(5, s)]
            bgb = [u_matmul(6, s), u_matmul(7, s)]
            for c in range(2):
                sc.op("act", lambda e, c=c: e.activation(xbuf[:, c, 3:3 + TB], banks[bx[c]][:], AF.Identity),
                      reads=[R_bank[bx[c]]], writes=[R_xbuf[c]])
                xc = pf32.get()
                lb = po + P_LB + c
                b4 = nbank()

                def emit_c4(e, c=c, b4=b4):
                    ins = None
                    for j in range(4):
                        i = 62 + c * 4 + j
                        ins = e.matmul(banks[b4][:], ring[:, DIAG_SLOTS[i // 24], (i % 24) * 128:(i % 24 + 1) * 128],
                                       xbuf[:, c, j:j + TB], start=(j == 0), stop=(j == 3))
                    return ins
                sc.op("pe", emit_c4, reads=[R_slot[x] for x in DIAG_SLOTS] + [R_xbuf[c]], writes=[R_bank[b4]],
                      cost=4 * 0.222)
                sc.op("act", lambda e, xc=xc, b4=b4, lb=lb: e.activation(
                    xc.t[:], banks[b4][:], AF.Identity, bias=params[:, lb:lb + 1]),
                    reads=[R_bank[b4], R_const], writes=[xc.r])
                sc.op("dve", lambda e, c=c: e.tensor_copy(xbuf[:, c, 0:3], xbuf[:, c, TB:TB + 3]),
                      reads=[R_xbuf[c]], writes=[R_xbuf[c]], n=3)
                xcb = pb16.get()
                sc.op("act", lambda e, xc=xc, xcb=xcb: e.activation(xcb.t[:], xc.t[:], AF.Identity),
                      reads=[xc.r], writes=[xcb.r])
                bra = nbank()
                sc.op("pe", lambda e, xcb=xcb, bra=bra, c=c: e.matmul(
                    banks[bra][:], bdb[:, bo + c * 128:bo + (c + 1) * 128], xcb.t[:], start=True, stop=True),
                    reads=[xcb.r, R_const], writes=[R_bank[bra]], cost=0.25)
                brx = nbank()
                sc.op("pe", lambda e, xcb=xcb, brx=brx, c=c: e.matmul(
                    banks[brx][:], bdb[:, bo + 256 + c * 128:bo + 256 + (c + 1) * 128], xcb.t[:],
                    start=True, stop=True),
                    reads=[xcb.r, R_const], writes=[R_bank[brx]], cost=0.25)
                rr = pf32.get()
                pba = po + P_BA + c
                pbx = po + P_BX + c
                sc.op("act", lambda e, rr=rr, bra=bra, pba=pba: e.activation(
                    rr.t[:], banks[bra][:], AF.Sigmoid, bias=params[:, pba:pba + 1]),
                    reads=[R_bank[bra], R_const], writes=[rr.r])
                ii = pf32.get()
                sc.op("act", lambda e, ii=ii, brx=brx, pbx=pbx: e.activation(
                    ii.t[:], banks[brx][:], AF.Sigmoid, bias=params[:, pbx:pbx + 1]),
                    reads=[R_bank[brx], R_const], writes=[ii.r])
                aa = pf32.get()
                cl = do + 48 + c
                cl2 = do + 50 + c
                sc.op("act", lambda e, aa=aa, rr=rr, cl=cl: e.activation(
                    aa.t[:], rr.t[:], AF.Exp, scale=dparams[:, cl:cl + 1]),
                    reads=[rr.r, R_dpar], writes=[aa.r])
                sc.op("act", lambda e, rr=rr, cl2=cl2: e.activation(
                    rr.t[:], rr.t[:], AF.Exp, scale=dparams[:, cl2:cl2 + 1]),
                    reads=[rr.r, R_dpar], writes=[rr.r])
                sc.op("dve", lambda e, rr=rr: e.tensor_scalar(rr.t[:], rr.t[:], -1.0, 1.0, ALU.mult, ALU.add),
                      reads=[rr.r], writes=[rr.r])
                sc.op("act", lambda e, rr=rr: e.activation(rr.t[:], rr.t[:], AF.Sqrt),
                      reads=[rr.r], writes=[rr.r])
                sc.op("dve", lambda e, ii=ii, xc=xc: e.tensor_tensor(ii.t[:], ii.t[:], xc.t[:], ALU.mult),
                      reads=[ii.r, xc.r], writes=[ii.r])
                sc.op("dve", lambda e, ii=ii, rr=rr: e.tensor_tensor(ii.t[:], ii.t[:], rr.t[:], ALU.mult),
                      reads=[ii.r, rr.r], writes=[ii.r])
                hh = pf32.get()
                if s == 0:
                    sc.op("dve", lambda e, hh=hh, aa=aa, ii=ii: e.tensor_tensor_scan(
                        hh.t[:], aa.t[:], ii.t[:], 0.0, ALU.mult, ALU.add),
                        reads=[aa.r, ii.r], writes=[hh.r])
                else:
                    sc.op("dve", lambda e, hh=hh, aa=aa, ii=ii, c=c: e.tensor_tensor_scan(
                        hh.t[:], aa.t[:], ii.t[:], hlast[:, c:c + 1], ALU.mult, ALU.add),
                        reads=[aa.r, ii.r, R_hlast[c]], writes=[hh.r])
                sc.op("dve", lambda e, hh=hh, c=c: e.tensor_copy(hlast[:, c:c + 1], hh.t[:, TB - 1:TB]),
                      reads=[hh.r], writes=[R_hlast[c]], n=1)
                gsb = aa
                gs = pf32.get()
                sc.op("act", lambda e, gs=gs, c=c: e.activation(gs.t[:], banks[bgb[c]][:], AF.Identity),
                      reads=[R_bank[bgb[c]]], writes=[gs.r])
                t2 = pf32.get()
                sc.op("dve", lambda e, t2=t2, gs=gs: e.tensor_tensor(t2.t[:], gs.t[:], gs.t[:], ALU.mult),
                      reads=[gs.r], writes=[t2.r])
                sc.op("dve", lambda e, t2=t2: e.tensor_scalar(t2.t[:], t2.t[:], 0.044715, 1.0, ALU.mult, ALU.add),
                      reads=[t2.r], writes=[t2.r])
                sc.op("dve", lambda e, t2=t2, gs=gs: e.tensor_tensor(t2.t[:], t2.t[:], gs.t[:], ALU.mult),
                      reads=[t2.r, gs.r], writes=[t2.r])
                sc.op("act", lambda e, t2=t2: e.activation(t2.t[:], t2.t[:], AF.Sigmoid, scale=GELU_C),
                      reads=[t2.r], writes=[t2.r])
                sc.op("dve", lambda e, t2=t2, gs=gs: e.tensor_tensor(t2.t[:], t2.t[:], gs.t[:], ALU.mult),
                      reads=[t2.r, gs.r], writes=[t2.r])
                sc.op("dve", lambda e, t2=t2, hh=hh, c=c: e.tensor_tensor(hy[:, 2 + c, :], t2.t[:], hh.t[:], ALU.mult),
                      reads=[t2.r, hh.r], writes=[RY(s, 2 + c)])


        def _attn_branch(l, s, po, do, bo):
            for c in range(4):
                b = u_matmul(8 + c, s)
                sc.op("act", lambda e, b=b, c=c: e.activation(qsb[:, c, :], banks[b][:], AF.Identity, scale=0.125),
                      reads=[R_bank[b]], writes=[R_qsb])
            b = u_matmul(12, s)
            sc.op("act", lambda e, b=b: e.activation(ksb[:, blk(s)], banks[b][:], AF.Identity),
                  reads=[R_bank[b]], writes=[R_ksb[s]])
            bv = nbank()
            u_prefetch(13, s)
            vslot, voff = u_chunk_ap(13, 0, s)

            def emit_v(e, bv=bv):
                ins = None
                for tt in range(4):
                    T = s * 4 + tt
                    for k in range(KC):
                        slot, off = u_chunk_ap(13, k, s)
                        ins = e.matmul(banks[bv][:, tt * 128:(tt + 1) * 128], xb[:, k, T * 128:(T + 1) * 128],
                                       ring[:, slot, off:off + 128], start=(k == 0), stop=(k == KC - 1))
                return ins
            sc.op("pe", emit_v, reads=[R_slot[vslot]] + [R_xb[k][s] for k in range(KC)], writes=[R_bank[bv]], cost=2.2)
            sc.op("dve", lambda e, bv=bv: e.tensor_copy(
                vsb[:, s * 4:(s + 1) * 4, :, 0:64],
                banks[bv][:].rearrange("p (t g d) -> p t g d", t=4, g=2)),
                reads=[R_bank[bv]], writes=[R_vsb[s]])

            for tt in range(4):
                T = s * 4 + tt
                first = (T == 0)
                ob = [nbank(), nbank()]
                kprev_res = R_ksb[(T - 1) // 4] if not first else None
                vprev_res = R_vsb[(T - 1) // 4] if not first else None
                for c0 in (0, 2):
                    sbk = [nbank(), nbank()]

                    def emit_s(e, sbk=sbk, c0=c0, T=T, first=first, tt=tt):
                        ins = None
                        for ci in range(2):
                            for hh_ in range(2):
                                pr = slice(hh_ * 64, (hh_ + 1) * 64)
                                if not first:
                                    ins = e.matmul(banks[sbk[hh_]][:, ci * 256:ci * 256 + 128],
                                                   ksb[pr, (T - 1) * 128:T * 128],
                                                   qsb[pr, c0 + ci, tt * 128:(tt + 1) * 128], start=True, stop=True)
                                ins = e.matmul(banks[sbk[hh_]][:, ci * 256 + 128:ci * 256 + 256],
                                               ksb[pr, T * 128:(T + 1) * 128],
                                               qsb[pr, c0 + ci, tt * 128:(tt + 1) * 128], start=True, stop=True)
                        return ins
                    rds = [R_qsb, R_ksb[s]] + ([kprev_res] if kprev_res is not None else [])
                    sc.op("pe", emit_s, reads=rds, writes=[R_bank[sbk[0]], R_bank[sbk[1]]], cost=0.6)
                    pts = []
                    for hh_ in range(2):
                        ex = pf32.get()
                        pt = pb16.get()
                        eo = hh_ * 1024 + c0 * 256
                        if first:
                            sc.op("act", lambda e, ex=ex, hh_=hh_, sbk=sbk: e.activation(
                                ex.t[:].rearrange("p (h a q) -> p h a q", h=2, a=2)[:, :, 1, :],
                                banks[sbk[hh_]][:].rearrange("p (h a q) -> p h a q", h=2, a=2)[:, :, 1, :], AF.Exp),
                                reads=[R_bank[sbk[hh_]]], writes=[ex.r])
                            sc.op("dve", lambda e, ex=ex, pt=pt, eo=eo: e.tensor_tensor(
                                pt.t[:].rearrange("p (h a q) -> p h a q", h=2, a=2)[:, :, 1, :],
                                ex.t[:].rearrange("p (h a q) -> p h a q", h=2, a=2)[:, :, 1, :],
                                EB[:, eo:eo + 512].rearrange("p (h a q) -> p h a q", h=2, a=2)[:, :, 1, :],
                                ALU.mult),
                                reads=[ex.r, R_const], writes=[pt.r])
                        else:
                            sc.op("act", lambda e, ex=ex, hh_=hh_, sbk=sbk: e.activation(
                                ex.t[:], banks[sbk[hh_]][:], AF.Exp),
                                reads=[R_bank[sbk[hh_]]], writes=[ex.r])
                            sc.op("dve", lambda e, ex=ex, pt=pt, eo=eo: e.tensor_tensor(
                                pt.t[:], ex.t[:], EB[:, eo:eo + 512], ALU.mult),
                                reads=[ex.r, R_const], writes=[pt.r])
                        pts.append(pt)

                    def emit_o(e, pts=pts, c0=c0, T=T, first=first, ob=ob):
                        ins = None
                        for ci in range(2):
                            for hh_ in range(2):
                                o_ap = banks[ob[hh_]][:, (c0 + ci) * 65:(c0 + ci + 1) * 65]
                                pt = pts[hh_]
                                if not first:
                                    e.matmul(o_ap, pt.t[:, ci * 256:ci * 256 + 128], vsb[:, T - 1, hh_, :],
                                             start=True, stop=False)
                                ins = e.matmul(o_ap, pt.t[:, ci * 256 + 128:ci * 256 + 256], vsb[:, T, hh_, :],
                                               start=first, stop=True)
                        return ins
                    rds = [pts[0].r, pts[1].r, R_vsb[s]] + ([vprev_res] if vprev_res is not None else [])
                    sc.op("pe", emit_o, reads=rds, writes=[R_bank[ob[0]], R_bank[ob[1]]], cost=0.45)
                otok = pb16.get()
                rd = rdp.get()
                ovs = []
                for hh_ in range(2):
                    ov = banks[ob[hh_]][:, 0:260].rearrange("p (c d) -> p c d", c=4)
                    ovs.append(ov)
                    sk = do + 52 + hh_ * 4
                    sc.op("dve", lambda e, ov=ov, hh_=hh_, sk=sk, rd=rd: e.tensor_tensor(
                        rd.t[:, hh_ * 4:(hh_ + 1) * 4].rearrange("p (c o) -> p c o", o=1), ov[:, :, 64:65],
                        dparams[:, sk:sk + 4].rearrange("p (c o) -> p c o", o=1), ALU.add),
                        reads=[R_bank[ob[hh_]], R_dpar], writes=[rd.r], n=8)
                sc.op("dve", lambda e, rd=rd: e.reciprocal(rd.t[:], rd.t[:]), reads=[rd.r], writes=[rd.r], n=8)
                for hh_ in range(2):
                    sc.op("dve", lambda e, ov=ovs[hh_], hh_=hh_, otok=otok, rd=rd: e.tensor_tensor(
                        otok.t[:, hh_ * 256:(hh_ + 1) * 256].rearrange("p (c d) -> p c d", c=4), ov[:, :, 0:64],
                        rd.t[:, hh_ * 4:(hh_ + 1) * 4].rearrange("p (c o) -> p c o", o=1).to_broadcast([128, 4, 64]),
                        ALU.mult),
                        reads=[R_bank[ob[hh_]], rd.r], writes=[otok.r], n=256)
                tb = nbank()

                def emit_t(e, otok=otok, tb=tb):
                    ins = None
                    tv = banks[tb][:].bitcast(BF16)
                    for i in range(4):
                        ins = e.transpose(tv[:, i * 128:(i + 1) * 128], otok.t[:, i * 128:(i + 1) * 128],
                                          identb[:])
                    return ins
                sc.op("pe", emit_t, reads=[otok.r, R_const], writes=[R_bank[tb]], cost=0.35)
                sc.op("act", lambda e, tb=tb, tt=tt: e.activation(
                    hy[:, 4:8, tt * 128:(tt + 1) * 128],
                    banks[tb][:].bitcast(BF16)[:, 0:512].rearrange("p (i q) -> p i q", i=4), AF.Identity),
                    reads=[R_bank[tb]], writes=[RY(s, 4), RY(s, 5), RY(s, 6), RY(s, 7)])

        def wout_block(l, s):
            for m in range(KC):
                b = nbank()

                def emit(e, m=m, b=b):
                    lst = []
                    for kc in range(8):
                        p, loc = divmod(kc, 3)
                        lst.append((banks[b][:], ring[:, WOUT_SLOTS[p], loc * 1024 + m * 128:loc * 1024 + (m + 1) * 128],
                                    hy[:, kc, :]))
                    return _mm_group(e, lst)
                sc.op("pe", emit, reads=[R_slot[x] for x in WOUT_SLOTS] + [RY(s, c_) for c_ in range(8)],
                      writes=[R_bank[b]], cost=1.78)
                sc.op("dve", lambda e, m=m, b=b: e.tensor_tensor(
                    xr[:, m, blk(s)], banks[b][:], xr[:, m, blk(s)], ALU.add),
                    reads=[R_bank[b], R_xr[m][s]], writes=[R_xr[m][s]])

        def dump_xr():
            for k in range(KC):
                for s in range(NB):
                    sc.dma("sp", lambda e, k=k, s=s: e.dma_start(out=yout[k * 128:(k + 1) * 128, blk(s)],
                                                                 in_=xr[:, k, blk(s)]),
                           "out", reads=[R_xr[k][s]], writes=[R_out])

        for l in range(n_layers):
            last_layer = (l == n_layers - 1)

            def run_ffn(l, f, ln_i, final, next_loader):
                NG = len(GROUPS)
                pend = None
                for g in range(NG):
                    for s in range(NB):
                        hb = hcount[0] % 2
                        hcount[0] += 1
                        ffn_up(g, s, hb)
                        if pend is not None:
                            ffn_down(*pend)
                            if pend[0] == NG - 1:
                                ln_block(l, ln_i, pend[1], final)
                            if pend[0] == g - 1:
                                next_loader(g - 1)
                        pend = (g, s, hb)
                ffn_down(*pend)
                ln_block(l, ln_i, pend[1], final)
                next_loader(NG - 1)

            def loader_f1(g, l=l):
                if g + 2 < len(GROUPS):
                    load_ffn_group(l, 0, g + 2)
                elif g == len(GROUPS) - 2:
                    ensure_win(l, 0, 0)
                    load_diag(l)
                    if len(GROUPS[-1]) < 4:
                        ensure_win(l, 0, 1)
                else:
                    load_wout(l)

            sc.mark(f"L{l}.ffn1")
            run_ffn(l, 0, 0, False, loader_f1)
            sc.mark(f"L{l}.mixer")
            if stop_after == "ffn1":
                dump_xr()
                break
            sc.op("dve", lambda e: e.memset(ybuf[:, :, 0:30], 0.0), writes=R_ybuf)
            sc.op("dve", lambda e: e.memset(xbuf[:, :, 0:3], 0.0), writes=R_xbuf)
            for s in range(NB):
                mixer_block(l, s)
                if s == NB - 1:
                    load_ffn_group(l, 1, 0)
                wout_block(l, s)
                if ydbg is not None:
                    for c in range(8):
                        sc.dma("pool", lambda e, c=c, s=s: e.dma_start(out=ydbg[c * 128:(c + 1) * 128, blk(s)],
                                                                       in_=hy[:, c, :]),
                               "out", reads=[R_hy[c]], writes=[R_out])
                ln_block(l, 1, s, False)
            if stop_after == "mixer":
                dump_xr()
                break
            load_ffn_group(l, 1, 1)

            def loader_f2(g, l=l):
                if g + 2 < len(GROUPS):
                    load_ffn_group(l, 1, g + 2)
                elif not (l == n_layers - 1):
                    load_ffn_group(l + 1, 0, g + 2 - len(GROUPS))

            sc.mark(f"L{l}.ffn2")
            run_ffn(l, 1, 2, last_layer, loader_f2)

        import time as _time
        _t = _time.time()
        sc.schedule()
        print("sched stats:", {e: len(sc.order[e]) for e in ENGS}, "nf32", nf32,
              "makespan_us %.0f" % sc.makespan, "sched_s %.1f" % (_time.time() - _t))
        if SCHED_VERBOSE:
            mk = sc.marks + [("end", len(sc.ops))]
            for (nm, a), (_, b) in zip(mk[:-1], mk[1:]):
                seg = sc.ops[a:b]
                if not seg:
                    continue
                busy = {e: sum(o.cost for o in seg if o.eng == e and o.dsem is None) for e in ("pe", "act", "dve")}
                print("  phase %-10s start %7.0f end %7.0f  busy" % (nm, min(o.fin - o.cost for o in seg), max(o.fin for o in seg)),
                      {e: round(v) for e, v in busy.items()})
        if SIM_ONLY:
            return nc
        engs = {}
        with nc.Block() as block:
            @block.tensor
            def _(e):
                engs["pe"] = e
                sc.emit_all_one("pe", e)

            @block.vector
            def _(e):
                sc.emit_all_one("dve", e)

            @block.scalar
            def _(e):
                sc.emit_all_one("act", e)

            @block.gpsimd
            def _(e):
                sc.emit_all_one("pool", e)

            @block.sync
            def _(e):
                sc.emit_all_one("sp", e)
    return nc


def _rel_bucket_np(dist):
    max_exact = 16
    d = np.maximum(dist, 1).astype(np.float32)
    large = max_exact + (np.log(d / np.float32(max_exact)) / np.float32(np.log(128 / max_exact))
                         * np.float32(32 - max_exact)).astype(np.int32)
    large = np.minimum(large, 31)
    return np.where(dist < max_exact, dist, large)


def _prep_shared(inp):
    f32 = np.float32
    wst = np.zeros((DEPTH * UNITS_PER_LAYER, 128, SLOT_E), f32)
    qperm = []
    for c in range(4):
        qperm += list(range(1024 + c * 64, 1024 + (c + 1) * 64))
        qperm += list(range(1024 + (4 + c) * 64, 1024 + (5 + c) * 64))
    cols = list(range(1024)) + qperm + list(range(1536, 1792))
    for l in range(DEPTH):
        base = l * UNITS_PER_LAYER
        for f in range(2):
            wg = np.asarray(inp["ffn_w_gate"][l, f], f32).reshape(KC, 128, NCH, 128)
            wu = np.asarray(inp["ffn_w_up"][l, f], f32).reshape(KC, 128, NCH, 128)
            wd = np.asarray(inp["ffn_w_down"][l, f], f32).reshape(NCH, 128, D)
            u = wst[base + f * NCH: base + (f + 1) * NCH]
            u[:, :, 0:1024] = wg.transpose(2, 1, 0, 3).reshape(NCH, 128, 1024)
            u[:, :, 1024:2048] = wu.transpose(2, 1, 0, 3).reshape(NCH, 128, 1024)
            u[:, :, 2048:3072] = wd
        win = np.asarray(inp["w_in"][l], f32)[:, cols].reshape(KC, 128, 14, 128)
        winr = win.transpose(2, 1, 0, 3).reshape(14, 128, 1024)
        for cc in range(14):
            p, loc = divmod(cc, 3)
            wst[base + 2 * NCH + p, :, loc * 1024:(loc + 1) * 1024] = winr[cc]
        wo = np.asarray(inp["w_out"][l], f32).reshape(KC, 128, D)
        for kc in range(KC):
            p, loc = divmod(kc, 3)
            wst[base + 2 * NCH + 5 + p, :, loc * 1024:(loc + 1) * 1024] = wo[kc]
    for l in range(DEPTH):
        base = l * UNITS_PER_LAYER + 2 * NCH + 8
        cw = np.asarray(inp["conv_dw_w"][l], f32)
        ar = np.arange(128)
        for c in range(2):
            for j in range(31):
                i = c * 31 + j
                wst[base + i // 24, ar, (i % 24) * 128 + ar] = cw[j, c * 128:(c + 1) * 128]
        lw = np.asarray(inp["lru_conv_w"][l], f32)
        for c in range(2):
            for j in range(4):
                i = 62 + c * 4 + j
                wst[base + i // 24, ar, (i % 24) * 128 + ar] = lw[j, c * 128:(c + 1) * 128]
    P = np.zeros((128, DEPTH * NPL), f32)
    BD = np.zeros((128, DEPTH * 512), f32)
    for l in range(DEPTH):
        o = l * NPL
        P[:, o + P_LNG:o + P_LNG + 24] = np.asarray(inp["ln_g"][l], f32).reshape(3, 8, 128).transpose(2, 0, 1).reshape(128, 24)
        P[:, o + P_LNB:o + P_LNB + 24] = np.asarray(inp["ln_b"][l], f32).reshape(3, 8, 128).transpose(2, 0, 1).reshape(128, 24)
        P[:, o + P_CW:o + P_CW + 62] = np.asarray(inp["conv_dw_w"][l], f32).reshape(31, 2, 128).transpose(2, 1, 0).reshape(128, 62)
        P[:, o + P_CB:o + P_CB + 2] = np.asarray(inp["conv_dw_b"][l], f32).reshape(2, 128).T
        P[:, o + P_CG:o + P_CG + 2] = np.asarray(inp["conv_ln_g"][l], f32).reshape(2, 128).T
        P[:, o + P_CBT:o + P_CBT + 2] = np.asarray(inp["conv_ln_b"][l], f32).reshape(2, 128).T
        P[:, o + P_LW:o + P_LW + 8] = np.asarray(inp["lru_conv_w"][l], f32).reshape(4, 2, 128).transpose(2, 1, 0).reshape(128, 8)
        P[:, o + P_LB:o + P_LB + 2] = np.asarray(inp["lru_conv_b"][l], f32).reshape(2, 128).T
        P[:, o + P_BA:o + P_BA + 2] = np.asarray(inp["lru_ba"][l], f32).reshape(2, 128).T
        P[:, o + P_BX:o + P_BX + 2] = np.asarray(inp["lru_bx"][l], f32).reshape(2, 128).T
        P[:, o + P_LAM:o + P_LAM + 2] = np.asarray(inp["lru_lambda"][l], f32).reshape(2, 128).T
        P[:, o + P_SINK:o + P_SINK + 8] = np.broadcast_to(np.asarray(inp["attn_sinks"][l], f32)[None, :], (128, 8))
        for ax, nm in enumerate(("lru_wa", "lru_wx")):
            w = np.asarray(inp[nm][l], f32)
            for c in range(2):
                for hh in range(2):
                    BD[hh * 64:(hh + 1) * 64, l * 512 + ax * 256 + c * 128 + hh * 64: l * 512 + ax * 256 + c * 128 + (hh + 1) * 64] = w[2 * c + hh]
    rb = np.asarray(inp["rel_bias"], f32)
    kj = np.arange(128)[:, None]
    qi = np.arange(128)[None, :]
    biasT = np.zeros((128, 2, 4, 2, 128), f32)
    mask = np.zeros((128, 2, 128), f32)
    for part in range(2):
        dist = qi - kj + (128 if part == 0 else 0)
        valid = (dist >= 0) & (dist < 128)
        bucket = _rel_bucket_np(np.maximum(dist, 0))
        mask[:, part, :] = valid.astype(f32)
        for c in range(4):
            for hh in range(2):
                h = c + 4 * hh
                biasT[:, hh, c, part, :] = rb[bucket, h]
    return {
        "wst": wst, "prm": P, "bdd": BD,
        "biasT": np.ascontiguousarray(biasT.reshape(128, 2048)),
        "mask": np.ascontiguousarray(mask.reshape(128, 256)),
        "ident": np.eye(128, dtype=f32),
    }


_NC_CACHE = {}


def _get_nc(n_layers):
    if n_layers not in _NC_CACHE:
        _NC_CACHE[n_layers] = build_program(n_layers)
    return _NC_CACHE[n_layers]


FUSED = True


def kernel(**inputs):
    sh = _prep_shared(inputs)
    x = np.asarray(inputs["x"], np.float32)
    xT = [np.ascontiguousarray(x[b].T) for b in range(8)]
    if FUSED:
        nc = _get_nc(DEPTH)
        in_maps = [dict(sh, xin=xT[b]) for b in range(8)]
        res = run_bass_kernel_spmd(nc, in_maps, core_ids=list(range(8)))
        outs = [res.results[b]["yout"] for b in range(8)]
    else:
        nc = _get_nc(1)
        cur = xT
        for l in range(DEPTH):
            shl = dict(sh)
            shl["wst"] = np.ascontiguousarray(sh["wst"][l * UNITS_PER_LAYER:(l + 1) * UNITS_PER_LAYER])
            shl["prm"] = np.ascontiguousarray(sh["prm"][:, l * NPL:(l + 1) * NPL])
            shl["bdd"] = np.ascontiguousarray(sh["bdd"][:, l * 512:(l + 1) * 512])
            in_maps = [dict(shl, xin=cur[b]) for b in range(8)]
            res = run_bass_kernel_spmd(nc, in_maps, core_ids=list(range(8)))
            cur = [np.ascontiguousarray(res.results[b]["yout"]) for b in range(8)]
        outs = cur
    return np.stack([np.ascontiguousarray(o.T) for o in outs], axis=0).astype(np.float32)
```
